# Optimizing a Trainium2 kernel written in Bass

```python
import jax, jax.numpy as jnp
from jax import lax
import numpy as np

D_MODEL = 1024
BATCH = 8
SEQ = 4096
DEPTH = 4

HEAD_DIM = 64
SB_HEADS = D_MODEL // HEAD_DIM
SW_Q_HEADS = D_MODEL // HEAD_DIM
SW_KV_HEADS = max(1, SW_Q_HEADS // 8)
SW_GROUP = SW_Q_HEADS // SW_KV_HEADS
SW_IN = (SW_Q_HEADS + 2 * SW_KV_HEADS) * HEAD_DIM
WINDOW = 128
BLOCK = 128
ROPE_THETA = 500000.0
ROT_DIM = HEAD_DIM // 4
D_FF = ((8 * D_MODEL // 3 + 255) // 256) * 256
N_MIXERS = 2
N_SB_LAYERS = (DEPTH + 1) // 2
N_SW_LAYERS = DEPTH // 2
EPS = 1e-6

kernel_name = 'hybrid_stickbreak_swa_macaron'


def rms_norm(x, gain):
    xf = x.astype(jnp.float32)
    y = xf * lax.rsqrt(jnp.mean(xf * xf, axis=-1, keepdims=True) + EPS)
    return (y * gain.astype(jnp.float32)).astype(x.dtype)


def swiglu(h, w_gate_up, w_down):
    gate, up = jnp.split(h @ w_gate_up, 2, axis=-1)
    return (jax.nn.silu(gate) * up) @ w_down


def rope_angles(positions):
    inv_freq = ROPE_THETA ** (-jnp.arange(0, ROT_DIM, 2, dtype=jnp.float32) / ROT_DIM)
    ang = positions.astype(jnp.float32)[..., None] * inv_freq
    return jnp.cos(ang), jnp.sin(ang)


def apply_partial_rope(x, cos, sin):
    half = ROT_DIM // 2
    x1 = x[..., :half].astype(jnp.float32)
    x2 = x[..., half:ROT_DIM].astype(jnp.float32)
    rot = jnp.concatenate([x1 * cos - x2 * sin, x2 * cos + x1 * sin], axis=-1).astype(x.dtype)
    return jnp.concatenate([rot, x[..., ROT_DIM:]], axis=-1)


def stick_breaking_attention(h, w_in, w_out):
    b, s, _ = h.shape
    q, k, v = jnp.split(h @ w_in, 3, axis=-1)
    q = q.reshape(b, s, SB_HEADS, HEAD_DIM).transpose(0, 2, 1, 3)
    k = k.reshape(b, s, SB_HEADS, HEAD_DIM).transpose(0, 2, 1, 3)
    v = v.reshape(b, s, SB_HEADS, HEAD_DIM).transpose(0, 2, 1, 3)
    scale = HEAD_DIM ** -0.5
    outs = []
    for blk in range(s // BLOCK):
        t0 = blk * BLOCK
        kv_len = t0 + BLOCK
        qb = q[:, :, t0:kv_len]
        kb = k[:, :, :kv_len]
        vb = v[:, :, :kv_len]
        z = jnp.einsum('bhtd,bhsd->bhts', qb, kb).astype(jnp.float32) * scale
        causal = (np.arange(kv_len)[None, :] < (t0 + np.arange(BLOCK))[:, None])
        log_beta = jax.nn.log_sigmoid(z)
        log_keep = jnp.where(causal, log_beta - z, 0.0)
        later = lax.cumsum(log_keep, axis=3, reverse=True) - log_keep
        weights = jnp.where(causal, jnp.exp(log_beta + later), 0.0)
        outs.append(jnp.einsum('bhts,bhsd->bhtd', weights.astype(vb.dtype), vb))
    o = jnp.concatenate(outs, axis=2).transpose(0, 2, 1, 3).reshape(b, s, SB_HEADS * HEAD_DIM)
    return o @ w_out


def sliding_window_attention(h, positions, w_in, w_out, q_gain, k_gain, sinks):
    b, s, _ = h.shape
    nb = s // BLOCK
    qkv = h @ w_in
    nq = SW_Q_HEADS * HEAD_DIM
    nk = SW_KV_HEADS * HEAD_DIM
    q = qkv[..., :nq].reshape(b, s, SW_KV_HEADS, SW_GROUP, HEAD_DIM)
    k = qkv[..., nq:nq + nk].reshape(b, s, SW_KV_HEADS, HEAD_DIM)
    v = qkv[..., nq + nk:].reshape(b, s, SW_KV_HEADS, HEAD_DIM)
    q = rms_norm(q, q_gain)
    k = rms_norm(k, k_gain)
    cos, sin = rope_angles(positions)
    q = apply_partial_rope(q, cos[:, :, None, None, :], sin[:, :, None, None, :])
    k = apply_partial_rope(k, cos[:, :, None, :], sin[:, :, None, :])
    pad = jnp.zeros((b, BLOCK, SW_KV_HEADS, HEAD_DIM), k.dtype)
    k_pad = jnp.concatenate([pad, k], axis=1).reshape(b, nb + 1, BLOCK, SW_KV_HEADS, HEAD_DIM)
    v_pad = jnp.concatenate([pad, v], axis=1).reshape(b, nb + 1, BLOCK, SW_KV_HEADS, HEAD_DIM)
    k_band = jnp.concatenate([k_pad[:, :-1], k_pad[:, 1:]], axis=2)
    v_band = jnp.concatenate([v_pad[:, :-1], v_pad[:, 1:]], axis=2)
    qb = q.reshape(b, nb, BLOCK, SW_KV_HEADS, SW_GROUP, HEAD_DIM)
    scores = jnp.einsum('bntkgd,bnskd->bnkgts', qb, k_band).astype(jnp.float32) * (HEAD_DIM ** -0.5)
    blk_start = np.arange(nb)[:, None, None] * BLOCK
    q_pos = blk_start + np.arange(BLOCK)[None, :, None]
    k_pos = blk_start - BLOCK + np.arange(2 * BLOCK)[None, None, :]
    mask = (k_pos >= 0) & (k_pos <= q_pos) & (q_pos - k_pos < WINDOW)
    scores = jnp.where(mask[None, :, None, None], scores, -jnp.inf)
    sink = sinks.astype(jnp.float32).reshape(SW_KV_HEADS, SW_GROUP)[None, None, :, :, None, None]
    m = jnp.maximum(jnp.max(scores, axis=-1, keepdims=True), sink)
    p = jnp.exp(scores - m)
    probs = p / (jnp.sum(p, axis=-1, keepdims=True) + jnp.exp(sink - m))
    o = jnp.einsum('bnkgts,bnskd->bntkgd', probs.astype(v_band.dtype), v_band)
    return o.reshape(b, s, SW_Q_HEADS * HEAD_DIM) @ w_out


def setup_inputs(seed: int = 0) -> dict:
    key = jax.random.key(seed)
    ks = jax.random.split(key, 14)
    f32 = jnp.float32
    x = jax.random.normal(ks[0], (BATCH, SEQ, D_MODEL), f32)
    offsets = jax.random.randint(ks[1], (BATCH, 1), 0, 1024, dtype=jnp.int32)
    positions = offsets + jnp.arange(SEQ, dtype=jnp.int32)[None, :]
    norm_gains = 1.0 + 0.02 * jax.random.normal(ks[2], (DEPTH, 3, D_MODEL), f32)
    ffn_w_gate_up = jax.random.normal(ks[3], (DEPTH, 2, D_MODEL, 2 * D_FF), f32) * D_MODEL ** -0.5
    ffn_w_down = jax.random.normal(ks[4], (DEPTH, 2, D_FF, D_MODEL), f32) * D_FF ** -0.5
    sb_w_in = jax.random.normal(ks[5], (N_SB_LAYERS, D_MODEL, 3 * SB_HEADS * HEAD_DIM), f32) * D_MODEL ** -0.5
    sb_w_out = jax.random.normal(ks[6], (N_SB_LAYERS, SB_HEADS * HEAD_DIM, D_MODEL), f32) * (SB_HEADS * HEAD_DIM) ** -0.5
    sw_w_in = jax.random.normal(ks[7], (N_SW_LAYERS, D_MODEL, SW_IN), f32) * D_MODEL ** -0.5
    sw_w_out = jax.random.normal(ks[8], (N_SW_LAYERS, SW_Q_HEADS * HEAD_DIM, D_MODEL), f32) * (SW_Q_HEADS * HEAD_DIM) ** -0.5
    sw_q_gain = 1.0 + 0.02 * jax.random.normal(ks[9], (N_SW_LAYERS, HEAD_DIM), f32)
    sw_k_gain = 1.0 + 0.02 * jax.random.normal(ks[10], (N_SW_LAYERS, HEAD_DIM), f32)
    sw_sinks = 0.5 * jax.random.normal(ks[11], (N_SW_LAYERS, SW_Q_HEADS), f32)
    return {'x': x, 'positions': positions, 'norm_gains': norm_gains,
            'ffn_w_gate_up': ffn_w_gate_up, 'ffn_w_down': ffn_w_down,
            'sb_w_in': sb_w_in, 'sb_w_out': sb_w_out,
            'sw_w_in': sw_w_in, 'sw_w_out': sw_w_out,
            'sw_q_gain': sw_q_gain, 'sw_k_gain': sw_k_gain, 'sw_sinks': sw_sinks}


def reference(x, positions, norm_gains, ffn_w_gate_up, ffn_w_down, sb_w_in, sb_w_out,
              sw_w_in, sw_w_out, sw_q_gain, sw_k_gain, sw_sinks):
    for i in range(DEPTH):
        slot = i // N_MIXERS
        x = x + 0.5 * swiglu(rms_norm(x, norm_gains[i, 0]), ffn_w_gate_up[i, 0], ffn_w_down[i, 0])
        h = rms_norm(x, norm_gains[i, 1])
        if i % N_MIXERS == 0:
            x = x + stick_breaking_attention(h, sb_w_in[slot], sb_w_out[slot])
        else:
            x = x + sliding_window_attention(h, positions, sw_w_in[slot], sw_w_out[slot],
                                             sw_q_gain[slot], sw_k_gain[slot], sw_sinks[slot])
        x = x + 0.5 * swiglu(rms_norm(x, norm_gains[i, 2]), ffn_w_gate_up[i, 1], ffn_w_down[i, 1])
    return x
```

```python
import numpy as np
import ml_dtypes
import concourse.bass as bass
import concourse.mybir as mybir
from concourse.bass_utils import run_bass_kernel_spmd

F32 = mybir.dt.float32
BF16 = mybir.dt.bfloat16
I32 = mybir.dt.int32
AF = mybir.ActivationFunctionType
ALU = mybir.AluOpType

S = 4096
D = 1024
DFF = 2816
NL = 4
NT = 8
EPS = 1e-6
NEG = -4096.0
PI = float(np.pi)
TWO_PI = float(2 * np.pi)

C_ID, C_TRI, C_NEG1, C_ONE, C_BLK, C_SWP, C_E0, C_E1 = [i * 128 for i in range(8)]
C_SBM = 8 * 128
C_SWM = C_SBM + 4 * 512
NCB = C_SWM + 512


class Tok:
    __slots__ = ("name", "w", "r", "sem", "cnt", "bg")

    def __init__(self, name="", bg=False):
        self.name = name
        self.w = None
        self.r = []
        self.sem = None
        self.cnt = 0
        self.bg = bg


class Eng:
    def __init__(self, name):
        self.name = name
        self.ops = []
        self.n = 0
        self.seen = {}
        self.sem = None
        self.needs_inc = set()


class FW:
    def __init__(self, nc):
        self.nc = nc
        self.E = {n: Eng(n) for n in ("pe", "act", "dve", "pool", "sp")}
        for e in self.E.values():
            e.sem = nc.alloc_semaphore("esem_" + e.name)
        self.nsem = 0
        self.dma_toks = {}
        self.t_bar = Tok("barrier")
        self.bar_fn = None

    def _wait(self, eng, ev):
        if ev is None:
            return
        key = ev[1]
        val = ev[2]
        if ev[0] == "e" and key == eng.name and eng.name == "pe":
            return
        if eng.seen.get(key, -1) >= val:
            return
        eng.seen[key] = val
        eng.ops.append(("w", ev))
        if ev[0] == "e":
            self.E[key].needs_inc.add(val)

    def _deps(self, eng, reads, writes):
        for t in reads:
            self._wait(eng, t.w)
        for t in writes:
            self._wait(eng, t.w)
            for ev in t.r:
                self._wait(eng, ev)

    def op(self, engname, fn, reads=(), writes=()):
        eng = self.E[engname]
        self._deps(eng, reads, writes)
        iid = eng.n
        eng.n += 1
        ev = ("e", engname, iid)
        eng.ops.append(("i", fn, iid))
        for t in reads:
            t.r = [e for e in t.r if not (e[0] == "e" and e[1] == engname)] + [ev]
        for t in writes:
            t.w = ev
            t.r = []
        return ev

    def dma(self, engname, fn, reads=(), writes=(), nodeps=False, nowaw=False):
        eng = self.E[engname]
        if nowaw:
            for t in reads:
                self._wait(eng, t.w)
            for t in writes:
                for ev in t.r:
                    self._wait(eng, ev)
        elif not nodeps:
            self._deps(eng, reads, writes)
        tok = writes[0]
        if tok.sem is None:
            tok.sem = self.nc.alloc_semaphore("dsem_%d" % self.nsem)
            self.nsem += 1
        tok.cnt += 16
        ev = ("d", tok, tok.cnt)
        eng.ops.append(("d", fn, tok))
        if not tok.bg:
            self.dma_toks[id(tok)] = tok
        for t in reads:
            t.r = [e for e in t.r if not (e[0] == "d" and e[1] is tok)] + [ev]
        for t in writes:
            t.w = ev
            t.r = []
        return ev

    def barrier(self):
        sp = self.E["sp"]
        for n, e in self.E.items():
            if n != "sp" and e.n > 0:
                self._wait(sp, ("e", n, e.n - 1))
        for tok in self.dma_toks.values():
            self._wait(sp, ("d", tok, tok.cnt))
        self.dma_toks = {}
        ev = self.dma("sp", self.bar_fn, writes=[self.t_bar], nodeps=True)
        self.dma_toks = {}
        for n, e in self.E.items():
            if n != "sp":
                self._wait(e, ev)
        self._wait(sp, ev)

    def finalize(self):
        nc = self.nc
        valmap = {}
        for e in self.E.values():
            c = 0
            for o in e.ops:
                if o[0] == "i" and o[2] in e.needs_inc:
                    c += 1
                    valmap[(e.name, o[2])] = c
        E = self.E

        def replay(e, h):
            for o in e.ops:
                if o[0] == "w":
                    ev = o[1]
                    if ev[0] == "e":
                        h.wait_ge(E[ev[1]].sem, valmap[(ev[1], ev[2])])
                    else:
                        h.wait_ge(ev[1].sem, ev[2])
                elif o[0] == "i":
                    ins = o[1](h)
                    if o[2] in e.needs_inc:
                        ins.then_inc(e.sem, 1)
                else:
                    ins = o[1](h)
                    ins.then_inc(o[2].sem, 16)

        with nc.Block() as block:
            @block.tensor
            def _(h):
                replay(E["pe"], h)

            @block.scalar
            def _(h):
                replay(E["act"], h)

            @block.vector
            def _(h):
                replay(E["dve"], h)

            @block.gpsimd
            def _(h):
                replay(E["pool"], h)

            @block.sync
            def _(h):
                replay(E["sp"], h)


class Builder:
    def __init__(self, n_passes=None, layers=(0, 1, 2, 3)):
        self.n_passes = n_passes
        self.layers = list(layers)
        self.nsb = sum(1 for l in self.layers if l % 2 == 0)
        self.nsw = sum(1 for l in self.layers if l % 2 == 1)
        NLg = len(self.layers)
        self.pass_count = 0
        nc = bass.Bass("TRN2", target_bir_lowering=False)
        self.nc = nc
        fw = FW(nc)
        self.fw = fw

        def din(name, shape, dt):
            return nc.dram_tensor(name, shape, dt, kind="ExternalInput").ap()

        def dint(name, shape, dt):
            return nc.dram_tensor(name, shape, dt, kind="Internal").ap()

        self.xT = din("xT", [D, S], F32)
        self.pos = din("pos", [1, S], I32)
        self.wgu = din("wgu", [2 * NLg, D, 2 * DFF], F32)
        self.wd = din("wd", [2 * NLg, DFF, D], F32)
        if self.nsb:
            self.sbin = din("sbin", [self.nsb, D, 3072], F32)
            self.sbout = din("sbout", [self.nsb, D, D], F32)
        if self.nsw:
            self.swin = din("swin", [self.nsw, D, 1280], F32)
            self.swout = din("swout", [self.nsw, D, D], F32)
        self.gains_d = din("gains", [128, 96], F32)
        self.qkg_d = din("qkg", [128, 4], F32)
        self.sinks_d = din("sinks", [128, 16], F32)
        self.cstb_d = din("cstb", [128, NCB], BF16)
        self.cstf_d = din("cstf", [128, 4], F32)
        self.yT = nc.dram_tensor("yT", [D, S], F32, kind="ExternalOutput").ap()

        self.wgu_b = dint("wgu_b", [2 * NLg, D, 2 * DFF], BF16)
        self.wd_b = dint("wd_b", [2 * NLg, DFF, D], BF16)
        if self.nsb:
            self.sbin_b = dint("sbin_b", [self.nsb, D, 3072], BF16)
            self.sbout_b = dint("sbout_b", [self.nsb, D, D], BF16)
        if self.nsw:
            self.swin_b = dint("swin_b", [self.nsw, D, 1280], BF16)
            self.swout_b = dint("swout_b", [self.nsw, D, D], BF16)
        self.qT_d = dint("qT_d", [8, 128, S], BF16)
        self.kT_d = dint("kT_d", [8, 128, S], BF16)
        self.v_d = dint("v_d", [S, D], BF16)
        self.v2_d = dint("v2_d", [S, 128], BF16)
        self.oT_d = dint("oT_d", [8, 128, S], BF16)
        self.cos_d = dint("cos_d", [128, S], F32)
        self.sin_d = dint("sin_d", [128, S], F32)
        self.bar_d = dint("bar_d", [2, 64], F32)
        fw.bar_fn = lambda h: h.dma_start(out=self.bar_d[0:1, :], in_=self.bar_d[1:2, :])

        tx = Tok("X")
        self.t_X = [tx for j in range(NT)]
        tf = [Tok("ffnw%d" % i, bg=True) for i in range(8)]
        tm = [Tok("mixw%d" % i, bg=True) for i in range(4)]
        self.t_wgu_b = tf
        self.t_wd_b = tf
        self.t_sbin_b = [tm[0], tm[2]]
        self.t_sbout_b = [tm[0], tm[2]]
        self.t_swin_b = [tm[1], tm[3]]
        self.t_swout_b = [tm[1], tm[3]]
        tq, tk, to = Tok("qT"), Tok("kT"), Tok("oT")
        self.t_qT = [tq for i in range(8)]
        self.t_kT = [tk for i in range(8)]
        self.t_v = Tok("v")
        self.t_oT = [to for i in range(8)]
        self.t_cs = Tok("cossin")

        A = nc.alloc_sbuf_tensor
        self.cst = A("cst", [128, NCB], BF16)
        self.cstf = A("cstf_s", [128, 4], F32)
        self.gains = A("gains_s", [128, 96], F32)
        self.qkg = A("qkg_s", [128, 4], F32)
        self.sinkexp = A("sinkexp", [128, 16], F32)
        self.t_c = Tok("consts")
        self.xt = [A("xt%d" % i, [128, 8, 512], F32) for i in range(2)]
        self.t_xt = [[Tok("xt%d_%d" % (i, k)) for k in range(8)] for i in range(4)]
        self.xt = self.xt + [None, None]
        self.h = [A("h%d" % i, [128, 8, 512], BF16) for i in range(2)]
        self.t_h = [[Tok("h%d_%d" % (i, k)) for k in range(8)] for i in range(2)]
        self.sq = [A("sq%d" % i, [128, 512], BF16) for i in range(4)]
        self.t_sq = [Tok("sq%d" % i) for i in range(4)]
        self.rt = [A("rt%d" % i, [128, 512], F32) for i in range(2)]
        self.t_rt = [Tok("rt%d" % i) for i in range(2)]
        self.rstd = [A("rstd%d" % i, [128, 512], F32) for i in range(2)]
        self.t_rstd = [Tok("rstd%d" % i) for i in range(2)]
        self.ARENA_W = 31000
        self.arena = A("arena", [128, self.ARENA_W], F32)
        self.PS = [nc.alloc_psum_tensor("ps%d" % i, [128, 512], F32) for i in range(8)]
        self.t_ps = [Tok("ps%d" % i) for i in range(8)]
        self.norm_ctr = 0
        self.ev_ctr = 0

    def carve_reset(self):
        self.aoff = 0
        self.tok_ctr = {}

    def T(self, pfx):
        if not hasattr(self, "tok_cache"):
            self.tok_cache = {}
        n = self.tok_ctr.get("all", 0)
        self.tok_ctr["all"] = n + 1
        key = ("all", n)
        if key not in self.tok_cache:
            self.tok_cache[key] = Tok("%s%d" % (pfx, n))
        return self.tok_cache[key]

    def carve(self, shape, dt):
        n = int(np.prod(shape[1:]))
        words = n if dt == F32 else (n + 1) // 2
        assert self.aoff + words <= self.ARENA_W, "arena overflow %d" % (self.aoff + words)
        v = self.arena[:, self.aoff:self.aoff + words]
        self.aoff += words
        if dt != F32:
            v = v.bitcast(dt)
        if len(shape) == 3:
            v = v.rearrange("p (a b) -> p a b", a=shape[1])
        elif len(shape) == 4:
            v = v.rearrange("p (a b c) -> p a b c", a=shape[1], b=shape[2])
        return v

    def mm(self, out, lhsT, rhs, start, stop, reads, writes, skip=False):
        self.fw.op("pe", lambda h: h.matmul(out, lhsT=lhsT, rhs=rhs, start=start, stop=stop,
                                            skip_group_check=skip), reads, writes)

    def act(self, out, in_, func, reads, writes, scale=1.0, bias=0.0):
        self.fw.op("act", lambda h: h.activation(out=out, in_=in_, func=func, bias=bias, scale=scale),
                   reads, writes)

    def tt(self, out, in0, in1, op, reads, writes, eng="dve"):
        self.fw.op(eng, lambda h: h.tensor_tensor(out=out, in0=in0, in1=in1, op=op), reads, writes)

    def ts(self, out, in0, s1, op0, reads, writes, s2=None, op1=None, eng="dve"):
        if op1 is None:
            self.fw.op(eng, lambda h: h.tensor_scalar(out=out, in0=in0, scalar1=s1, scalar2=None, op0=op0),
                       reads, writes)
        else:
            self.fw.op(eng, lambda h: h.tensor_scalar(out=out, in0=in0, scalar1=s1, scalar2=s2, op0=op0, op1=op1),
                       reads, writes)

    def stt(self, out, in0, scalar, in1, op0, op1, reads, writes, eng="dve"):
        self.fw.op(eng, lambda h: h.scalar_tensor_tensor(out=out, in0=in0, scalar=scalar, in1=in1, op0=op0, op1=op1),
                   reads, writes)

    def cp(self, out, in_, reads, writes, eng="dve"):
        if eng == "act":
            self.act(out, in_, AF.Copy, reads, writes)
        else:
            self.fw.op(eng, lambda h: h.tensor_copy(out=out, in_=in_), reads, writes)

    def recip(self, out, in_, reads, writes):
        self.fw.op("dve", lambda h: h.reciprocal(out=out, in_=in_), reads, writes)

    def memset(self, ap, val, writes, eng="pool"):
        self.fw.op(eng, lambda h: h.memset(ap, val), (), writes)

    def ld(self, out, in_, reads, writes, eng="sp", nodeps=False, nowaw=False):
        return self.fw.dma(eng, lambda h: h.dma_start(out=out, in_=in_), reads, writes, nodeps=nodeps, nowaw=nowaw)

    def evac(self, out, in_, reads, writes):
        self.ev_ctr += 1
        self.cp(out, in_, reads, writes, eng=("act" if self.ev_ctr % 2 else "dve"))

    def cb(self, c0, n=128):
        return self.cst[:, c0:c0 + n]

    def prologue(self):
        nc = self.nc
        self.ld(self.cst[:, :], self.cstb_d, (), [self.t_c])
        self.ld(self.cstf[:, :], self.cstf_d, (), [self.t_c])
        self.ld(self.gains[:, :], self.gains_d, (), [self.t_c])
        self.ld(self.qkg[:, :], self.qkg_d, (), [self.t_c])
        self.ld(self.sinkexp[:, :], self.sinks_d, (), [self.t_c])
        self.act(self.sinkexp[:, :], self.sinkexp[:, :], AF.Exp, [self.t_c], [self.t_c])
        def cast(dst, src, tok, rows, step):
            for r0 in range(0, rows, step):
                self.ld(dst[r0:r0 + step, :], src[r0:r0 + step, :], (), [tok], eng="pool", nodeps=True)
        isb = isw = 0
        for idx, li in enumerate(self.layers):
            cast(self.wgu_b[2 * idx], self.wgu[2 * idx], self.t_wgu_b[2 * idx], D, 128)
            cast(self.wd_b[2 * idx], self.wd[2 * idx], self.t_wd_b[2 * idx], DFF, 704)
            if li % 2 == 0:
                cast(self.sbin_b[isb], self.sbin[isb], self.t_sbin_b[isb], D, 256)
                cast(self.sbout_b[isb], self.sbout[isb], self.t_sbout_b[isb], D, 512)
                isb += 1
            else:
                cast(self.swin_b[isw], self.swin[isw], self.t_swin_b[isw], D, 512)
                cast(self.swout_b[isw], self.swout[isw], self.t_swout_b[isw], D, 512)
                isw += 1
            cast(self.wgu_b[2 * idx + 1], self.wgu[2 * idx + 1], self.t_wgu_b[2 * idx + 1], D, 128)
            cast(self.wd_b[2 * idx + 1], self.wd[2 * idx + 1], self.t_wd_b[2 * idx + 1], DFF, 704)
        self.carve_reset()
        pint = [self.arena[:, i * 512:(i + 1) * 512].bitcast(I32) for i in range(2)]
        self.aoff = 1024
        pf = [self.carve([128, 512], F32) for _ in range(2)]
        ang = [self.carve([128, 512], F32) for _ in range(2)]
        kf = [self.carve([128, 512], F32) for _ in range(2)]
        ki = [self.carve([128, 512], F32).bitcast(I32) for _ in range(2)]
        gt = [self.carve([128, 512], F32) for _ in range(2)]
        res = [self.carve([128, 512], F32) for _ in range(4)]
        t_pi = [Tok() for _ in range(2)]
        t_pf = [Tok() for _ in range(2)]
        t_ang = [Tok() for _ in range(2)]
        t_kf = [Tok() for _ in range(2)]
        t_ki = [Tok() for _ in range(2)]
        t_gt = [Tok() for _ in range(2)]
        t_res = [Tok() for _ in range(4)]
        invf = self.cstf[:, 0:1]
        for j in range(NT):
            s = j % 2
            self.ld(pint[s], self.pos[:, j * 512:(j + 1) * 512].broadcast_to([128, 512]), (), [t_pi[s]])
            self.cp(pf[s], pint[s], [t_pi[s]], [t_pf[s]])
            for which in range(2):
                r = (2 * j + which) % 4
                if which == 0:
                    self.ts(ang[s], pf[s], invf, ALU.mult, [t_pf[s], self.t_c], [t_ang[s]])
                else:
                    self.ts(ang[s], pf[s], invf, ALU.mult, [t_pf[s], self.t_c], [t_ang[s]], s2=PI / 2, op1=ALU.add)
                self.ts(kf[s], ang[s], 1.0 / TWO_PI, ALU.mult, [t_ang[s]], [t_kf[s]])
                self.cp(ki[s], kf[s], [t_kf[s]], [t_ki[s]])
                self.cp(kf[s], ki[s], [t_ki[s]], [t_kf[s]])
                self.stt(ang[s], kf[s], -TWO_PI, ang[s], ALU.mult, ALU.add, [t_kf[s], t_ang[s]], [t_ang[s]])
                self.ts(gt[s], ang[s], PI, ALU.is_gt, [t_ang[s]], [t_gt[s]], s2=-TWO_PI, op1=ALU.mult)
                self.tt(ang[s], ang[s], gt[s], ALU.add, [t_ang[s], t_gt[s]], [t_ang[s]])
                self.ts(ang[s], ang[s], -PI, ALU.max, [t_ang[s]], [t_ang[s]], s2=PI, op1=ALU.min)
                self.act(res[r], ang[s], AF.Sin, [t_ang[s]], [t_res[r]])
                dst = self.sin_d if which == 0 else self.cos_d
                self.ld(dst[:, j * 512:(j + 1) * 512], res[r], [t_res[r]], [self.t_cs], nowaw=True)
        self.fw.barrier()

    def X3(self, ap):
        return ap.rearrange("(kc p) t -> p kc t", p=128)

    def load_x(self, src, j, slot):
        self.ld(self.xt[slot][:, :, :], self.X3(src)[:, :, j * 512:(j + 1) * 512],
                [self.t_X[j]], self.t_xt[slot])

    def store_x(self, j, slot):
        self.ld(self.X3(self.yT)[:, :, j * 512:(j + 1) * 512], self.xt[slot][:, :, :],
                self.t_xt[slot], [self.t_X[j]], nowaw=True)

    def norm(self, slot, gidx, bank=7, hslot=None):
        if hslot is None:
            hslot = slot
        xt = self.xt[slot]
        ps = self.PS[bank]
        tps = self.t_ps[bank]
        n = self.norm_ctr
        self.norm_ctr += 1
        for kc in range(8):
            q = (n * 8 + kc) % 4
            self.act(self.sq[q][:, :], xt[:, kc, :], AF.Square, [self.t_xt[slot][kc]], [self.t_sq[q]])
            self.mm(ps[:, :], self.cb(C_ONE), self.sq[q][:, :], kc == 0, kc == 7, [self.t_sq[q], self.t_c], [tps])
        r = n % 2
        self.act(self.rt[r][:, :], ps[:, :], AF.Ln, [tps], [self.t_rt[r]], scale=1.0 / D, bias=EPS)
        self.act(self.rstd[r][:, :], self.rt[r][:, :], AF.Exp, [self.t_rt[r]], [self.t_rstd[r]], scale=-0.5)
        for kc in range(8):
            self.stt(self.h[hslot][:, kc, :], xt[:, kc, :], self.gains[:, gidx * 8 + kc:gidx * 8 + kc + 1],
                     self.rstd[r][:, :], ALU.mult, ALU.mult,
                     [self.t_xt[slot][kc], self.t_rstd[r], self.t_c], [self.t_h[hslot][kc]])

    def end_pass(self):
        self.fw.barrier()
        self.pass_count += 1
        return self.n_passes is not None and self.pass_count >= self.n_passes

    def ffn_pass(self, f, gidx, src):
        self.carve_reset()
        self.xt[2] = self.carve([128, 8, 512], F32)
        self.xt[3] = self.carve([128, 8, 512], F32)
        actT = [self.carve([128, 22, 512], BF16) for _ in range(2)]
        t_act = [[self.T("a") for _ in range(22)] for _ in range(2)]
        tmp = [self.carve([128, 512], F32) for _ in range(2)]
        t_tmp = [self.T("a") for _ in range(2)]
        NW = 2
        wg = [self.carve([128, 8, 256], BF16) for _ in range(NW)]
        wu = [self.carve([128, 8, 256], BF16) for _ in range(NW)]
        t_wg = [self.T("a") for _ in range(NW)]
        t_wu = [self.T("a") for _ in range(NW)]
        wdn = [self.carve([128, 22, 256], BF16) for _ in range(2)]
        t_wdn = [self.T("a") for _ in range(2)]
        wgu3 = self.wgu_b[f].rearrange("(kc p) n -> p kc n", p=128)
        wd3 = self.wd_b[f].rearrange("(c p) n -> p c n", p=128)
        PS, tps = self.PS, self.t_ps
        st_ = {"w": 0, "d": 0, "e": 0}

        def xs(T, s):
            return (0 if T % 2 == 0 else 2) + s

        def stageN(T):
            for s in range(2):
                self.load_x(src, 2 * T + s, xs(T, s))
                self.norm(xs(T, s), gidx, hslot=s)

        def stageU(T):
            for g in range(11):
                ws = st_["w"] % NW
                st_["w"] += 1
                self.ld(wg[ws], wgu3[:, :, g * 256:(g + 1) * 256], [self.t_wgu_b[f]], [t_wg[ws]])
                self.ld(wu[ws], wgu3[:, :, DFF + g * 256:DFF + (g + 1) * 256], [self.t_wgu_b[f]], [t_wu[ws]])
                for s in range(2):
                    for cc in range(2):
                        c = 2 * g + cc
                        e = st_["e"] % 2
                        st_["e"] += 1
                        bg, bu = e, 2 + e
                        for kc in range(8):
                            self.mm(PS[bg][:, :], wg[ws][:, kc, cc * 128:(cc + 1) * 128], self.h[s][:, kc, :],
                                    kc == 0, kc == 7, [t_wg[ws], self.t_h[s][kc]], [tps[bg]])
                        for kc in range(8):
                            self.mm(PS[bu][:, :], wu[ws][:, kc, cc * 128:(cc + 1) * 128], self.h[s][:, kc, :],
                                    kc == 0, kc == 7, [t_wu[ws], self.t_h[s][kc]], [tps[bu]])
                        self.act(tmp[e], PS[bg][:, :], AF.Silu, [tps[bg]], [t_tmp[e]])
                        self.tt(actT[s][:, c, :], tmp[e], PS[bu][:, :], ALU.mult, [t_tmp[e], tps[bu]], [t_act[s][c]])

        def stageD(T, dgs):
            for dg in dgs:
                ds_ = st_["d"] % 2
                st_["d"] += 1
                self.ld(wdn[ds_], wd3[:, :, dg * 256:(dg + 1) * 256], [self.t_wd_b[f]], [t_wdn[ds_]])
                for s in range(2):
                    xsl = xs(T, s)
                    for dd in range(2):
                        dc = 2 * dg + dd
                        e = st_["e"] % 2
                        st_["e"] += 1
                        by = 4 + e
                        for c in range(22):
                            self.mm(PS[by][:, :], wdn[ds_][:, c, dd * 128:(dd + 1) * 128], actT[s][:, c, :],
                                    c == 0, c == 21, [t_wdn[ds_], t_act[s][c]], [tps[by]])
                        self.stt(self.xt[xsl][:, dc, :], PS[by][:, :], 0.5, self.xt[xsl][:, dc, :], ALU.mult, ALU.add,
                                 [tps[by], self.t_xt[xsl][dc]], [self.t_xt[xsl][dc]])

        NG = NT // 2
        stageN(0)
        for T in range(NG):
            stageU(T)
            if T + 1 < NG:
                stageN(T + 1)
            stageD(T, [0, 1])
            stageD(T, [2, 3])
            for s in range(2):
                self.store_x(2 * T + s, xs(T, s))
        return self.end_pass()

    def outproj_pass(self, w_b, t_w):
        self.carve_reset()
        wout = self.carve([128, 8, 1024], BF16)
        t_wout = self.T("p6_")
        oTt = [self.carve([128, 8, 512], BF16) for _ in range(2)]
        t_oTt = [self.T("p7_") for _ in range(2)]
        self.ld(wout, w_b.rearrange("(c p) n -> p c n", p=128), [t_w], [t_wout])
        o3 = self.oT_d.rearrange("h p t -> p h t")
        PS, tps = self.PS, self.t_ps
        for j in range(NT):
            s = j % 2
            self.load_x(self.yT, j, s)
            self.ld(oTt[s], o3[:, :, j * 512:(j + 1) * 512], [self.t_oT[0]], [t_oTt[s]])
            for dc in range(8):
                b = dc % 4
                for c in range(8):
                    self.mm(PS[b][:, :], wout[:, c, dc * 128:(dc + 1) * 128], oTt[s][:, c, :], c == 0, c == 7,
                            [t_wout, t_oTt[s]], [tps[b]])
                self.tt(self.xt[s][:, dc, :], PS[b][:, :], self.xt[s][:, dc, :], ALU.add,
                        [tps[b], self.t_xt[s][dc]], [self.t_xt[s][dc]])
            self.store_x(j, s)
        return self.end_pass()

    def sb_qkv_pass(self, sl, gidx):
        self.carve_reset()
        w = [self.carve([128, 8, 512], BF16) for _ in range(2)]
        t_w = [self.T("p8_") for _ in range(2)]
        st = [self.carve([128, 4, 512], BF16) for _ in range(2)]
        t_st = [[self.T("p9_") for _ in range(4)] for _ in range(2)]
        win3 = self.sbin_b[sl].rearrange("(kc p) n -> p kc n", p=128)
        q3 = self.qT_d.rearrange("h p t -> p h t")
        k3 = self.kT_d.rearrange("h p t -> p h t")
        v3 = self.v_d.rearrange("(b s) f -> s b f", s=128)
        PS, tps = self.PS, self.t_ps
        wctr = 0
        bctr = 0
        for j in range(NT):
            s = j % 2
            self.load_x(self.yT, j, s)
            self.norm(s, gidx)
            for grp in range(6):
                ws = wctr % 2
                wctr += 1
                self.ld(w[ws], win3[:, :, grp * 512:(grp + 1) * 512], [self.t_sbin_b[sl]], [t_w[ws]])
                ss = ws
                for i4 in range(4):
                    b = bctr % 4
                    bctr += 1
                    if grp < 4:
                        for kc in range(8):
                            self.mm(PS[b][:, :], w[ws][:, kc, i4 * 128:(i4 + 1) * 128], self.h[s][:, kc, :],
                                    kc == 0, kc == 7, [t_w[ws], self.t_h[s][kc]], [tps[b]])
                    else:
                        for kc in range(8):
                            self.mm(PS[b][:, :], self.h[s][:, kc, i4 * 128:(i4 + 1) * 128], w[ws][:, kc, :],
                                    kc == 0, kc == 7, [t_w[ws], self.t_h[s][kc]], [tps[b]])
                    self.evac(st[ss][:, i4, :], PS[b][:, :], [tps[b]], [t_st[ss][i4]])
                tsl = slice(j * 512, (j + 1) * 512)
                if grp < 2:
                    self.ld(q3[:, grp * 4:(grp + 1) * 4, tsl], st[ss], t_st[ss], [self.t_qT[0]], nowaw=True)
                elif grp < 4:
                    g2 = grp - 2
                    self.ld(k3[:, g2 * 4:(g2 + 1) * 4, tsl], st[ss], t_st[ss], [self.t_kT[0]], nowaw=True)
                else:
                    g2 = grp - 4
                    self.ld(v3[:, j * 4:(j + 1) * 4, g2 * 512:(g2 + 1) * 512], st[ss], t_st[ss], [self.t_v], nowaw=True)
        return self.end_pass()

    def sb_attn_pass(self):
        self.carve_reset()
        qT = self.carve([128, S], BF16)
        kT = self.carve([128, S], BF16)
        vp2 = self.carve([128, 32 * 2 * 128], BF16)
        vp = vp2.rearrange("p (a b c) -> p a b c", a=32, b=2)
        oTh = self.carve([128, S], BF16)
        NS = 4
        et = [self.carve([128, 512], F32) for _ in range(NS)]
        sp = [self.carve([128, 512], BF16) for _ in range(NS)]
        arg = [self.carve([128, 512], F32) for _ in range(NS)]
        Am = [self.carve([128, 512], BF16) for _ in range(NS)]
        Rb = [self.carve([128, 512], F32) for _ in range(2)]
        t_q, t_k, t_vp = self.T("p10_"), self.T("p11_"), self.T("p12_")
        t_o = [self.T("p13_") for _ in range(8)]
        t_et = [self.T("p14_") for _ in range(NS)]
        t_sp = [self.T("p15_") for _ in range(NS)]
        t_arg = [self.T("p16_") for _ in range(NS)]
        t_A = [self.T("p17_") for _ in range(NS)]
        t_R = [self.T("p18_") for _ in range(2)]
        PS, tps = self.PS, self.t_ps
        ident = self.cb(C_ID)
        tri = self.cb(C_TRI)
        neg1 = self.cb(C_NEG1)
        v3 = self.v_d.rearrange("(b s) f -> s b f", s=128)
        self.memset(vp2, 0.0, [t_vp])
        tc_ = self.t_c

        for hp in range(8):
            self.ld(qT, self.qT_d[hp], [self.t_qT[hp]], [t_q])
            self.ld(kT, self.kT_d[hp], [self.t_kT[hp]], [t_k])
            for e in range(2):
                for b0 in range(0, 32, 8):
                    self.ld(vp[:, b0:b0 + 8, e, e * 64:(e + 1) * 64],
                            v3[:, b0:b0 + 8, (2 * hp + e) * 64:(2 * hp + e + 1) * 64], [self.t_v], [t_vp])
            units = []
            for qg in range(8):
                for e in range(2):
                    kmax = 4 * qg + 3
                    for kb in range(kmax, -1, -1):
                        units.append((qg, e, kb))
            NU = len(units)

            NZ = 4

            def c0_of(u):
                qg, e, kb = units[u]
                m = kb - 4 * qg
                return 128 * m if m > 0 else 0

            def s1(u):
                qg, e, kb = units[u]
                rows = slice(e * 64, (e + 1) * 64)
                zb = u % NZ
                diag = kb >= 4 * qg
                c0 = c0_of(u)
                self.mm(PS[zb][:, c0:512], kT[rows, kb * 128:(kb + 1) * 128], qT[rows, qg * 512 + c0:(qg + 1) * 512],
                        True, False, [t_k, t_q], [tps[zb]], skip=True)
                if diag:
                    m = kb - 4 * qg
                    self.mm(PS[zb][:, c0:512], ident, self.cst[:, C_SBM + m * 512 + c0:C_SBM + (m + 1) * 512],
                            False, False, [tc_], [tps[zb]], skip=True)

            def s2a(u):
                zb = u % NZ
                sl_ = u % NS
                c0 = c0_of(u)
                self.act(et[sl_][:, c0:512], PS[zb][:, c0:512], AF.Exp, [tps[zb]], [t_et[sl_]], scale=0.125)

            def s2b(u):
                sl_ = u % NS
                c0 = c0_of(u)
                self.act(sp[sl_][:, c0:512], et[sl_][:, c0:512], AF.Ln, [t_et[sl_]], [t_sp[sl_]], bias=1.0)

            def s3(u):
                qg, e, kb = units[u]
                zb = u % NZ
                rbk = 4 + u % 2
                sl_ = u % NS
                c0 = c0_of(u)
                self.mm(PS[zb][:, c0:512], tri, sp[sl_][:, c0:512], False, True, [tc_, t_sp[sl_]], [tps[zb]], skip=True)
                if kb > 0:
                    self.mm(PS[rbk][:, c0:512], neg1, sp[sl_][:, c0:512], True, True, [tc_, t_sp[sl_]], [tps[rbk]])

            def s4(u):
                qg, e, kb = units[u]
                zb = u % NZ
                rbk = 4 + u % 2
                sl_ = u % NS
                gi = (qg * 2 + e) % 2
                first = kb == 4 * qg + 3
                c0 = c0_of(u)
                if first:
                    self.memset(Rb[gi][:, 0:c0], 0.0, [t_R[gi]], eng="dve")
                    self.ts(PS[zb][:, c0:512], PS[zb][:, c0:512], 0.125, ALU.mult, [tps[zb]], [tps[zb]])
                    self.cp(Rb[gi][:, c0:512], PS[rbk][:, c0:512], [tps[rbk]], [t_R[gi]])
                else:
                    self.stt(PS[zb][:, c0:512], PS[zb][:, c0:512], 0.125, Rb[gi][:, c0:512], ALU.mult, ALU.add,
                             [tps[zb], t_R[gi]], [tps[zb]])
                    if kb > 0:
                        self.tt(Rb[gi][:, c0:512], Rb[gi][:, c0:512], PS[rbk][:, c0:512], ALU.add,
                                [t_R[gi], tps[rbk]], [t_R[gi]])

            def s5(u):
                sl_ = u % NS
                c0 = c0_of(u)
                zb = u % NZ
                self.act(Am[sl_][:, c0:512], PS[zb][:, c0:512], AF.Exp, [tps[zb]], [t_A[sl_]])

            def s6(u):
                qg, e, kb = units[u]
                sl_ = u % NS
                ob = 6 + qg % 2
                c0 = c0_of(u)
                first = (e == 0 and kb == 4 * qg + 3)
                last = (e == 1 and kb == 0)
                self.mm(PS[ob][:, c0:512], vp[:, kb, e, :], Am[sl_][:, c0:512], first, last, [t_vp, t_A[sl_]], [tps[ob]],
                        skip=True)
                if last:
                    self.evac(oTh[:, qg * 512:(qg + 1) * 512], PS[ob][:, :], [tps[ob]], [t_o[qg]])

            s1(0)
            for tau in range(NU + 3):
                if tau + 1 < NU:
                    s1(tau + 1)
                if tau < NU:
                    s2a(tau)
                if 0 <= tau - 2 < NU:
                    s5(tau - 2)
                if tau < NU:
                    s2b(tau)
                if 0 <= tau - 1 < NU:
                    s3(tau - 1)
                    s4(tau - 1)
                if 0 <= tau - 3 < NU:
                    s6(tau - 3)
            self.ld(self.oT_d[hp], oTh, t_o, [self.t_oT[hp]], nowaw=True)
        return self.end_pass()

    def sw_qkv_pass(self, sl, gidx, gsl):
        self.carve_reset()
        NSL = 3
        win = self.carve([128, 8, 1280], BF16)
        kdw = self.carve([128, 8, 256], BF16)
        t_win, t_kdw = self.T("w"), self.T("w")
        cs = [self.carve([128, 512], F32) for _ in range(2)]
        sn = [self.carve([128, 512], F32) for _ in range(2)]
        t_cs = [self.T("w") for _ in range(2)]
        t_sn = [self.T("w") for _ in range(2)]
        qn = [self.carve([128, 512], F32) for _ in range(NSL)]
        qnb = [self.carve([128, 512], BF16) for _ in range(NSL)]
        sqq = [self.carve([128, 512], BF16) for _ in range(NSL)]
        rq = [self.carve([128, 512], F32) for _ in range(NSL)]
        t1 = [self.carve([128, 512], F32) for _ in range(NSL)]
        t2 = [self.carve([128, 512], F32) for _ in range(NSL)]
        t_qn = [self.T("w") for _ in range(NSL)]
        t_qnb = [self.T("w") for _ in range(NSL)]
        t_sqq = [self.T("w") for _ in range(NSL)]
        t_rq = [self.T("w") for _ in range(NSL)]
        t_t1 = [self.T("w") for _ in range(NSL)]
        t_t2 = [self.T("w") for _ in range(NSL)]
        st = [self.carve([128, 4, 512], BF16) for _ in range(3)]
        t_st = [[self.T("w") for _ in range(4)] for _ in range(3)]
        vst = [self.carve([128, 4, 128], BF16) for _ in range(2)]
        t_vst = [self.T("w") for _ in range(2)]
        win3 = self.swin_b[sl].rearrange("(kc p) n -> p kc n", p=128)
        self.ld(win, win3, [self.t_swin_b[sl]], [t_win])
        for g in range(2):
            for dup in range(2):
                c0 = (g * 2 + dup) * 64
                self.ld(kdw[:, :, c0:c0 + 64], win3[:, :, 1024 + g * 64:1024 + (g + 1) * 64],
                        [self.t_swin_b[sl]], [t_kdw])
        q3 = self.qT_d.rearrange("h p t -> p h t")
        k3 = self.kT_d.rearrange("h p t -> p h t")
        v3 = self.v2_d.rearrange("(b s) f -> s b f", s=128)
        PS, tps = self.PS, self.t_ps
        blk = self.cb(C_BLK)
        swp = self.cb(C_SWP)
        tc_ = self.t_c
        units = [(j, u) for j in range(NT) for u in range(10)]
        NU = len(units)

        def prep_tile(j):
            s = j % 2
            tsl = slice(j * 512, (j + 1) * 512)
            self.load_x(self.yT, j, s)
            self.norm(s, gidx)
            self.ld(cs[s], self.cos_d[:, tsl], [self.t_cs], [t_cs[s]])
            self.ld(sn[s], self.sin_d[:, tsl], [self.t_cs], [t_sn[s]])

        def do_v(j):
            s = j % 2
            for i4 in range(4):
                for kc in range(8):
                    self.mm(PS[7][:, i4 * 128:(i4 + 1) * 128], self.h[s][:, kc, i4 * 128:(i4 + 1) * 128],
                            win[:, kc, 1152:1280], kc == 0, kc == 7, [t_win, self.t_h[s][kc]], [tps[7]])
            self.evac(vst[s].rearrange("p a b -> p (a b)"), PS[7][:, :], [tps[7]], [t_vst[s]])
            self.ld(v3[:, j * 4:(j + 1) * 4, :], vst[s], [t_vst[s]], [self.t_v], nowaw=True)

        def p1(i):
            j, unit = units[i]
            s = j % 2
            if unit == 0:
                prep_tile(j)
            u = i % NSL
            b0 = u
            isq = unit < 8
            for kc in range(8):
                lhs = win[:, kc, unit * 128:(unit + 1) * 128] if isq else kdw[:, kc, (unit - 8) * 128:(unit - 7) * 128]
                self.mm(PS[b0][:, :], lhs, self.h[s][:, kc, :], kc == 0, kc == 7,
                        [t_win if isq else t_kdw, self.t_h[s][kc]], [tps[b0]])
            self.act(sqq[u], PS[b0][:, :], AF.Square, [tps[b0]], [t_sqq[u]])
            if unit == 9:
                do_v(j)

        def p2(i):
            j, unit = units[i]
            u = i % NSL
            b0 = u
            b1 = 3 + i % 2
            isq = unit < 8
            gc = 2 * gsl + (0 if isq else 1)
            gcol = self.qkg[:, gc:gc + 1]
            self.mm(PS[b1][:, :], blk, sqq[u], True, True, [tc_, t_sqq[u]], [tps[b1]])
            self.act(rq[u], PS[b1][:, :], AF.Ln, [tps[b1]], [t_rq[u]], scale=1.0 / 64, bias=EPS)
            self.act(rq[u], rq[u], AF.Exp, [t_rq[u]], [t_rq[u]], scale=-0.5)
            self.stt(qn[u], PS[b0][:, :], gcol, rq[u], ALU.mult, ALU.mult, [tps[b0], t_rq[u], tc_], [t_qn[u]])
            self.cp(qnb[u], qn[u], [t_qn[u]], [t_qnb[u]], eng="act")

        def p3(i):
            j, unit = units[i]
            s = j % 2
            tsl = slice(j * 512, (j + 1) * 512)
            u = i % NSL
            b2 = 5 + i % 2
            isq = unit < 8
            self.mm(PS[b2][:, :], swp, qnb[u], True, True, [tc_, t_qnb[u]], [tps[b2]])
            self.tt(t1[u], qn[u], cs[s], ALU.mult, [t_qn[u], t_cs[s]], [t_t1[u]])
            self.tt(t2[u], PS[b2][:, :], sn[s], ALU.mult, [tps[b2], t_sn[s]], [t_t2[u]])
            if isq:
                ss = (unit // 4 + 2 * j) % 2
                i4 = unit % 4
                self.tt(st[ss][:, i4, :], t1[u], t2[u], ALU.add, [t_t1[u], t_t2[u]], [t_st[ss][i4]])
                if i4 == 3:
                    g4 = unit // 4
                    self.ld(q3[:, g4 * 4:(g4 + 1) * 4, tsl], st[ss], t_st[ss], [self.t_qT[0]], nowaw=True)
            else:
                ss = 2
                i4 = unit - 8
                self.tt(st[ss][:, i4, :], t1[u], t2[u], ALU.add, [t_t1[u], t_t2[u]], [t_st[ss][i4]])
                if i4 == 1:
                    self.ld(k3[:, 0:2, tsl], st[ss][:, 0:2, :], t_st[ss][0:2], [self.t_kT[0]], nowaw=True)

        for tau in range(NU + 2):
            if tau < NU:
                p1(tau)
            if 0 <= tau - 1 < NU:
                p2(tau - 1)
            if 0 <= tau - 2 < NU:
                p3(tau - 2)
        return self.end_pass()

    def sw_attn_pass(self, sl):
        self.carve_reset()
        qT = [self.carve([128, S], BF16) for _ in range(2)]
        t_q = [self.T("w") for _ in range(2)]
        kTd = self.carve([128, 2, S], BF16)
        t_k = self.T("w")
        vpad2 = [[self.carve([128, 32 * 128], BF16) for _ in range(2)] for _ in range(2)]
        vpad = [[vpad2[g][e].rearrange("p (a b) -> p a b", a=32) for e in range(2)] for g in range(2)]
        t_vp = self.T("w")
        oTh = [self.carve([128, S], BF16) for _ in range(2)]
        t_o = [[self.T("w") for _ in range(8)] for _ in range(2)]
        NA = 4
        Am = [self.carve([128, 512], BF16) for _ in range(NA)]
        t_A = [self.T("w") for _ in range(NA)]
        den = [self.carve([128, 512], F32) for _ in range(2)]
        t_den = [self.T("w") for _ in range(2)]
        PS, tps = self.PS, self.t_ps
        SBK = [0, 1, 6, 7]
        ident = self.cb(C_ID)
        tc_ = self.t_c
        swm = self.cst[:, C_SWM:C_SWM + 512]
        onesE = [self.cb(C_E0), self.cb(C_E1)]
        v3 = self.v2_d.rearrange("(b s) f -> s b f", s=128)
        k3 = self.kT_d.rearrange("h p t -> p h t")
        for g in range(2):
            for e in range(2):
                self.memset(vpad2[g][e], 0.0, [t_vp])
        self.ld(kTd, k3[:, 0:2, :], [self.t_kT[0]], [t_k])
        for g in range(2):
            for e in range(2):
                for b0 in range(0, 32, 8):
                    self.ld(vpad[g][e][:, b0:b0 + 8, e * 64:(e + 1) * 64], v3[:, b0:b0 + 8, g * 64:(g + 1) * 64],
                            [self.t_v], [t_vp])
        units = [(hp, kb, e) for hp in range(8) for kb in range(32) for e in range(2)]
        NU = len(units)

        def s1(i):
            hp, kb, e = units[i]
            qs = hp % 2
            g = hp // 4
            if kb == 0 and e == 0:
                self.ld(qT[qs], self.qT_d[hp], [self.t_qT[hp]], [t_q[qs]])
            bk = SBK[i % 4]
            a_ = i % NA
            n = 256 if kb < 31 else 128
            rows = slice(e * 64, (e + 1) * 64)
            self.mm(PS[bk][:, 0:n], kTd[rows, g, kb * 128:(kb + 1) * 128], qT[qs][rows, kb * 128:kb * 128 + n],
                    True, False, [t_k, t_q[qs]], [tps[bk]])
            self.mm(PS[bk][:, 0:n], ident, swm[:, 0:n], False, True, [tc_], [tps[bk]])
            self.act(Am[a_][:, 0:n], PS[bk][:, 0:n], AF.Exp, [tps[bk]], [t_A[a_]], scale=0.125)

        def s3(i):
            hp, kb, e = units[i]
            qs = hp % 2
            g = hp // 4
            a_ = i % NA
            for part in range(2):
                qb = kb + part
                if qb > 31:
                    continue
                qg = qb // 4
                ob = 2 + qg % 2
                db = 4 + qg % 2
                cols = slice((qb % 4) * 128, (qb % 4 + 1) * 128)
                first = (part == 1 and e == 0 and qb % 4 == 0) or (qb == 0 and part == 0 and e == 0)
                last = (part == 0 and e == 1)
                asl = Am[a_][:, part * 128:(part + 1) * 128]
                self.mm(PS[ob][:, cols], vpad[g][e][:, kb, :], asl, first, last, [t_vp, t_A[a_]], [tps[ob]], skip=True)
                self.mm(PS[db][:, cols], onesE[e], asl, first, last, [tc_, t_A[a_]], [tps[db]], skip=True)
            if e == 1 and kb % 4 == 3:
                qg = kb // 4
                ob = 2 + qg % 2
                db = 4 + qg % 2
                d_ = qg % 2
                self.ts(den[d_], PS[db][:, :], self.sinkexp[:, sl * 8 + hp:sl * 8 + hp + 1], ALU.add,
                        [tps[db], tc_], [t_den[d_]])
                self.recip(den[d_], den[d_], [t_den[d_]], [t_den[d_]])
                self.tt(oTh[qs][:, qg * 512:(qg + 1) * 512], PS[ob][:, :], den[d_], ALU.mult,
                        [tps[ob], t_den[d_]], [t_o[qs][qg]])
            if e == 1 and kb == 31:
                self.ld(self.oT_d[hp], oTh[qs], t_o[qs], [self.t_oT[hp]], nowaw=True)

        for tau in range(NU + 2):
            if tau < NU:
                s1(tau)
            if 0 <= tau - 2 < NU:
                s3(tau - 2)
        return self.end_pass()

    def build(self):
        self.prologue()
        isb = isw = 0
        for idx, li in enumerate(self.layers):
            gsl = li // 2
            src = self.xT if idx == 0 else self.yT
            if self.ffn_pass(2 * idx, li * 3 + 0, src):
                break
            if li % 2 == 0:
                if self.sb_qkv_pass(isb, li * 3 + 1):
                    break
                if self.sb_attn_pass():
                    break
                if self.outproj_pass(self.sbout_b[isb], self.t_sbout_b[isb]):
                    break
                isb += 1
            else:
                if self.sw_qkv_pass(isw, li * 3 + 1, gsl):
                    break
                if self.sw_attn_pass(gsl):
                    break
                if self.outproj_pass(self.swout_b[isw], self.t_swout_b[isw]):
                    break
                isw += 1
            if self.ffn_pass(2 * idx + 1, li * 3 + 2, self.yT):
                break
        self.fw.barrier()
        self.fw.finalize()
        return self.nc


def make_consts():
    bf = ml_dtypes.bfloat16
    c = np.zeros((128, NCB), np.float32)
    idx = np.arange(128)
    c[:, C_ID:C_ID + 128] = np.eye(128)
    c[:, C_TRI:C_TRI + 128] = np.where(idx[:, None] >= idx[None, :], -8.0, 0.0)
    c[:, C_NEG1:C_NEG1 + 128] = -1.0
    c[:, C_ONE:C_ONE + 128] = 1.0
    c[:, C_BLK:C_BLK + 128] = (idx[:, None] // 64 == idx[None, :] // 64).astype(np.float32)
    swp = np.zeros((128, 128), np.float32)
    for m in range(128):
        d = m % 64
        if d < 8:
            swp[m + 8, m] = -1.0
        elif d < 16:
            swp[m - 8, m] = 1.0
    c[:, C_SWP:C_SWP + 128] = swp
    c[:, C_E0:C_E0 + 128] = (idx[None, :] < 64).astype(np.float32)
    c[:, C_E1:C_E1 + 128] = (idx[None, :] >= 64).astype(np.float32)
    t = np.arange(512)
    for m in range(4):
        allowed = (m * 128 + idx[:, None]) < t[None, :]
        c[:, C_SBM + m * 512:C_SBM + (m + 1) * 512] = np.where(allowed, 0.0, NEG)
    tl = np.arange(128)
    cur = idx[:, None] <= tl[None, :]
    prev = idx[:, None] > tl[None, :]
    for rep in range(2):
        c[:, C_SWM + rep * 256:C_SWM + rep * 256 + 128] = np.where(cur, 0.0, NEG)
        c[:, C_SWM + rep * 256 + 128:C_SWM + rep * 256 + 256] = np.where(prev, 0.0, NEG)
    cf = np.zeros((128, 4), np.float32)
    inv_freq = (500000.0 ** (-np.arange(0, 16, 2, dtype=np.float32) / 16)).astype(np.float32)
    for p in range(128):
        d = p % 64
        if d < 16:
            cf[p, 0] = inv_freq[d % 8]
    return c.astype(bf), cf


_NC_CACHE = {}

PLAN = [[0, 1, 2, 3]]


def get_nc(layers, n_passes=None):
    key = (tuple(layers), n_passes)
    if key not in _NC_CACHE:
        _NC_CACHE[key] = Builder(n_passes, layers).build()
    return _NC_CACHE[key]


def make_in_maps(inputs, xTs, cores, layers):
    pos = np.asarray(inputs["positions"]).astype(np.int32)
    cstb, cstf = make_consts()
    ng = np.asarray(inputs["norm_gains"], np.float32)
    gains = np.ascontiguousarray(ng.reshape(12, 8, 128).transpose(2, 0, 1).reshape(128, 96))
    qg = np.asarray(inputs["sw_q_gain"], np.float32)
    kg = np.asarray(inputs["sw_k_gain"], np.float32)
    qkg = np.stack([np.tile(qg[0], 2), np.tile(kg[0], 2), np.tile(qg[1], 2), np.tile(kg[1], 2)], axis=1)
    sk = np.asarray(inputs["sw_sinks"], np.float32)
    sinks = np.zeros((128, 16), np.float32)
    for sl in range(2):
        for hp in range(8):
            sinks[:64, sl * 8 + hp] = sk[sl, 2 * hp]
            sinks[64:, sl * 8 + hp] = sk[sl, 2 * hp + 1]
    wgu = np.asarray(inputs["ffn_w_gate_up"], np.float32)
    wd = np.asarray(inputs["ffn_w_down"], np.float32)
    shared = {
        "wgu": np.ascontiguousarray(np.concatenate([wgu[l] for l in layers], axis=0)),
        "wd": np.ascontiguousarray(np.concatenate([wd[l] for l in layers], axis=0)),
        "gains": gains, "qkg": np.ascontiguousarray(qkg), "sinks": sinks, "cstb": cstb, "cstf": cstf,
    }
    sbl = [l // 2 for l in layers if l % 2 == 0]
    swl = [l // 2 for l in layers if l % 2 == 1]
    if sbl:
        shared["sbin"] = np.ascontiguousarray(np.asarray(inputs["sb_w_in"], np.float32)[sbl])
        shared["sbout"] = np.ascontiguousarray(np.asarray(inputs["sb_w_out"], np.float32)[sbl])
    if swl:
        shared["swin"] = np.ascontiguousarray(np.asarray(inputs["sw_w_in"], np.float32)[swl])
        shared["swout"] = np.ascontiguousarray(np.asarray(inputs["sw_w_out"], np.float32)[swl])
    maps = []
    for i, b in enumerate(cores):
        m = dict(shared)
        m["xT"] = xTs[i]
        m["pos"] = np.ascontiguousarray(pos[b].reshape(1, S))
        maps.append(m)
    return maps


def kernel(x, positions, norm_gains, ffn_w_gate_up, ffn_w_down, sb_w_in, sb_w_out,
           sw_w_in, sw_w_out, sw_q_gain, sw_k_gain, sw_sinks):
    inputs = dict(x=x, positions=positions, norm_gains=norm_gains, ffn_w_gate_up=ffn_w_gate_up,
                  ffn_w_down=ffn_w_down, sb_w_in=sb_w_in, sb_w_out=sb_w_out, sw_w_in=sw_w_in,
                  sw_w_out=sw_w_out, sw_q_gain=sw_q_gain, sw_k_gain=sw_k_gain, sw_sinks=sw_sinks)
    x = np.asarray(x)
    cores = list(range(8))
    xTs = [np.ascontiguousarray(x[b].T) for b in cores]
    for layers in PLAN:
        nc = get_nc(layers)
        maps = make_in_maps(inputs, xTs, cores, layers)
        res = run_bass_kernel_spmd(nc, maps, core_ids=cores)
        xTs = [np.asarray(res.results[b]["yT"]) for b in cores]
    out = np.empty((8, S, D), np.float32)
    for b in cores:
        out[b] = xTs[b].T
    return out
```

```python
import numpy as np
import ml_dtypes
import concourse.bass as bass
import concourse.mybir as mybir
from concourse.bass_utils import run_bass_kernel_spmd

F32 = mybir.dt.float32
BF16 = mybir.dt.bfloat16
I32 = mybir.dt.int32
AF = mybir.ActivationFunctionType
ALU = mybir.AluOpType

S = 4096
D = 1024
DFF = 2816
NL = 4
NT = 8
EPS = 1e-6
NEG = -4096.0
PI = float(np.pi)
TWO_PI = float(2 * np.pi)

C_ID, C_TRI, C_NEG1, C_ONE, C_BLK, C_SWP, C_E0, C_E1 = [i * 128 for i in range(8)]
C_SBM = 8 * 128
C_SWM = C_SBM + 4 * 512
NCB = C_SWM + 512


class Tok:
    __slots__ = ("name", "w", "r", "sem", "cnt", "bg")

    def __init__(self, name="", bg=False):
        self.name = name
        self.w = None
        self.r = []
        self.sem = None
        self.cnt = 0
        self.bg = bg


class Eng:
    def __init__(self, name):
        self.name = name
        self.ops = []
        self.n = 0
        self.seen = {}
        self.sem = None
        self.needs_inc = set()


class FW:
    def __init__(self, nc):
        self.nc = nc
        self.E = {n: Eng(n) for n in ("pe", "act", "dve", "pool", "sp")}
        for e in self.E.values():
            e.sem = nc.alloc_semaphore("esem_" + e.name)
        self.nsem = 0
        self.dma_toks = {}
        self.t_bar = Tok("barrier")
        self.bar_fn = None

    def _wait(self, eng, ev):
        if ev is None:
            return
        key = ev[1]
        val = ev[2]
        if ev[0] == "e" and key == eng.name and eng.name == "pe":
            return
        if eng.seen.get(key, -1) >= val:
            return
        eng.seen[key] = val
        eng.ops.append(("w", ev))
        if ev[0] == "e":
            self.E[key].needs_inc.add(val)

    def _deps(self, eng, reads, writes):
        for t in reads:
            self._wait(eng, t.w)
        for t in writes:
            self._wait(eng, t.w)
            for ev in t.r:
                self._wait(eng, ev)

    def op(self, engname, fn, reads=(), writes=()):
        eng = self.E[engname]
        self._deps(eng, reads, writes)
        iid = eng.n
        eng.n += 1
        ev = ("e", engname, iid)
        eng.ops.append(("i", fn, iid))
        for t in reads:
            t.r = [e for e in t.r if not (e[0] == "e" and e[1] == engname)] + [ev]
        for t in writes:
            t.w = ev
            t.r = []
        return ev

    def dma(self, engname, fn, reads=(), writes=(), nodeps=False, nowaw=False):
        eng = self.E[engname]
        if nowaw:
            for t in reads:
                self._wait(eng, t.w)
            for t in writes:
                for ev in t.r:
                    self._wait(eng, ev)
        elif not nodeps:
            self._deps(eng, reads, writes)
        tok = writes[0]
        if tok.sem is None:
            tok.sem = self.nc.alloc_semaphore("dsem_%d" % self.nsem)
            self.nsem += 1
        tok.cnt += 16
        ev = ("d", tok, tok.cnt)
        eng.ops.append(("d", fn, tok))
        if not tok.bg:
            self.dma_toks[id(tok)] = tok
        for t in reads:
            t.r = [e for e in t.r if not (e[0] == "d" and e[1] is tok)] + [ev]
        for t in writes:
            t.w = ev
            t.r = []
        return ev

    def barrier(self):
        sp = self.E["sp"]
        for n, e in self.E.items():
            if n != "sp" and e.n > 0:
                self._wait(sp, ("e", n, e.n - 1))
        for tok in self.dma_toks.values():
            self._wait(sp, ("d", tok, tok.cnt))
        self.dma_toks = {}
        ev = self.dma("sp", self.bar_fn, writes=[self.t_bar], nodeps=True)
        self.dma_toks = {}
        for n, e in self.E.items():
            if n != "sp":
                self._wait(e, ev)
        self._wait(sp, ev)

    def finalize(self):
        nc = self.nc
        valmap = {}
        for e in self.E.values():
            c = 0
            for o in e.ops:
                if o[0] == "i" and o[2] in e.needs_inc:
                    c += 1
                    valmap[(e.name, o[2])] = c
        E = self.E

        def replay(e, h):
            for o in e.ops:
                if o[0] == "w":
                    ev = o[1]
                    if ev[0] == "e":
                        h.wait_ge(E[ev[1]].sem, valmap[(ev[1], ev[2])])
                    else:
                        h.wait_ge(ev[1].sem, ev[2])
                elif o[0] == "i":
                    ins = o[1](h)
                    if o[2] in e.needs_inc:
                        ins.then_inc(e.sem, 1)
                else:
                    ins = o[1](h)
                    ins.then_inc(o[2].sem, 16)

        with nc.Block() as block:
            @block.tensor
            def _(h):
                replay(E["pe"], h)

            @block.scalar
            def _(h):
                replay(E["act"], h)

            @block.vector
            def _(h):
                replay(E["dve"], h)

            @block.gpsimd
            def _(h):
                replay(E["pool"], h)

            @block.sync
            def _(h):
                replay(E["sp"], h)


class Builder:
    def __init__(self, n_passes=None, layers=(0, 1, 2, 3)):
        self.n_passes = n_passes
        self.layers = list(layers)
        self.nsb = sum(1 for l in self.layers if l % 2 == 0)
        self.nsw = sum(1 for l in self.layers if l % 2 == 1)
        NLg = len(self.layers)
        self.pass_count = 0
        nc = bass.Bass("TRN2", target_bir_lowering=False)
        self.nc = nc
        fw = FW(nc)
        self.fw = fw

        def din(name, shape, dt):
            return nc.dram_tensor(name, shape, dt, kind="ExternalInput").ap()

        def dint(name, shape, dt):
            return nc.dram_tensor(name, shape, dt, kind="Internal").ap()

        self.xT = din("xT", [D, S], F32)
        self.pos = din("pos", [1, S], I32)
        self.wgu = din("wgu", [2 * NLg, D, 2 * DFF], F32)
        self.wd = din("wd", [2 * NLg, DFF, D], F32)
        if self.nsb:
            self.sbin = din("sbin", [self.nsb, D, 3072], F32)
            self.sbout = din("sbout", [self.nsb, D, D], F32)
        if self.nsw:
            self.swin = din("swin", [self.nsw, D, 1280], F32)
            self.swout = din("swout", [self.nsw, D, D], F32)
        self.gains_d = din("gains", [128, 96], F32)
        self.qkg_d = din("qkg", [128, 4], F32)
        self.sinks_d = din("sinks", [128, 16], F32)
        self.cstb_d = din("cstb", [128, NCB], BF16)
        self.cstf_d = din("cstf", [128, 4], F32)
        self.yT = nc.dram_tensor("yT", [D, S], F32, kind="ExternalOutput").ap()

        self.wgu_b = dint("wgu_b", [2 * NLg, D, 2 * DFF], BF16)
        self.wd_b = dint("wd_b", [2 * NLg, DFF, D], BF16)
        if self.nsb:
            self.sbin_b = dint("sbin_b", [self.nsb, D, 3072], BF16)
            self.sbout_b = dint("sbout_b", [self.nsb, D, D], BF16)
        if self.nsw:
            self.swin_b = dint("swin_b", [self.nsw, D, 1280], BF16)
            self.swout_b = dint("swout_b", [self.nsw, D, D], BF16)
        self.qT_d = dint("qT_d", [8, 128, S], BF16)
        self.kT_d = dint("kT_d", [8, 128, S], BF16)
        self.v_d = dint("v_d", [S, D], BF16)
        self.v2_d = dint("v2_d", [S, 128], BF16)
        self.oT_d = dint("oT_d", [8, 128, S], BF16)
        self.cos_d = dint("cos_d", [128, S], F32)
        self.sin_d = dint("sin_d", [128, S], F32)
        self.bar_d = dint("bar_d", [2, 64], F32)
        fw.bar_fn = lambda h: h.dma_start(out=self.bar_d[0:1, :], in_=self.bar_d[1:2, :])

        tx = Tok("X")
        self.t_X = [tx for j in range(NT)]
        tf = [Tok("ffnw%d" % i, bg=True) for i in range(8)]
        tm = [Tok("mixw%d" % i, bg=True) for i in range(4)]
        self.t_wgu_b = tf
        self.t_wd_b = tf
        self.t_sbin_b = [tm[0], tm[2]]
        self.t_sbout_b = [tm[0], tm[2]]
        self.t_swin_b = [tm[1], tm[3]]
        self.t_swout_b = [tm[1], tm[3]]
        tq, tk, to = Tok("qT"), Tok("kT"), Tok("oT")
        self.t_qT = [tq for i in range(8)]
        self.t_kT = [tk for i in range(8)]
        self.t_v = Tok("v")
        self.t_oT = [to for i in range(8)]
        self.t_cs = Tok("cossin")

        A = nc.alloc_sbuf_tensor
        self.cst = A("cst", [128, NCB], BF16)
        self.cstf = A("cstf_s", [128, 4], F32)
        self.gains = A("gains_s", [128, 96], F32)
        self.qkg = A("qkg_s", [128, 4], F32)
        self.sinkexp = A("sinkexp", [128, 16], F32)
        self.t_c = Tok("consts")
        self.xt = [A("xt%d" % i, [128, 8, 512], F32) for i in range(2)]
        self.t_xt = [[Tok("xt%d_%d" % (i, k)) for k in range(8)] for i in range(2)]
        self.h = [A("h%d" % i, [128, 8, 512], BF16) for i in range(2)]
        self.t_h = [[Tok("h%d_%d" % (i, k)) for k in range(8)] for i in range(2)]
        self.sq = [A("sq%d" % i, [128, 512], BF16) for i in range(4)]
        self.t_sq = [Tok("sq%d" % i) for i in range(4)]
        self.rt = [A("rt%d" % i, [128, 512], F32) for i in range(2)]
        self.t_rt = [Tok("rt%d" % i) for i in range(2)]
        self.rstd = [A("rstd%d" % i, [128, 512], F32) for i in range(2)]
        self.t_rstd = [Tok("rstd%d" % i) for i in range(2)]
        self.ARENA_W = 31000
        self.arena = A("arena", [128, self.ARENA_W], F32)
        self.PS = [nc.alloc_psum_tensor("ps%d" % i, [128, 512], F32) for i in range(8)]
        self.t_ps = [Tok("ps%d" % i) for i in range(8)]
        self.norm_ctr = 0
        self.ev_ctr = 0

    def carve_reset(self):
        self.aoff = 0
        self.tok_ctr = {}

    def T(self, pfx):
        if not hasattr(self, "tok_cache"):
            self.tok_cache = {}
        n = self.tok_ctr.get("all", 0)
        self.tok_ctr["all"] = n + 1
        key = ("all", n)
        if key not in self.tok_cache:
            self.tok_cache[key] = Tok("%s%d" % (pfx, n))
        return self.tok_cache[key]

    def carve(self, shape, dt):
        n = int(np.prod(shape[1:]))
        words = n if dt == F32 else (n + 1) // 2
        assert self.aoff + words <= self.ARENA_W, "arena overflow %d" % (self.aoff + words)
        v = self.arena[:, self.aoff:self.aoff + words]
        self.aoff += words
        if dt != F32:
            v = v.bitcast(dt)
        if len(shape) == 3:
            v = v.rearrange("p (a b) -> p a b", a=shape[1])
        elif len(shape) == 4:
            v = v.rearrange("p (a b c) -> p a b c", a=shape[1], b=shape[2])
        return v

    def mm(self, out, lhsT, rhs, start, stop, reads, writes, skip=False):
        self.fw.op("pe", lambda h: h.matmul(out, lhsT=lhsT, rhs=rhs, start=start, stop=stop,
                                            skip_group_check=skip), reads, writes)

    def act(self, out, in_, func, reads, writes, scale=1.0, bias=0.0):
        self.fw.op("act", lambda h: h.activation(out=out, in_=in_, func=func, bias=bias, scale=scale),
                   reads, writes)

    def tt(self, out, in0, in1, op, reads, writes, eng="dve"):
        self.fw.op(eng, lambda h: h.tensor_tensor(out=out, in0=in0, in1=in1, op=op), reads, writes)

    def ts(self, out, in0, s1, op0, reads, writes, s2=None, op1=None, eng="dve"):
        if op1 is None:
            self.fw.op(eng, lambda h: h.tensor_scalar(out=out, in0=in0, scalar1=s1, scalar2=None, op0=op0),
                       reads, writes)
        else:
            self.fw.op(eng, lambda h: h.tensor_scalar(out=out, in0=in0, scalar1=s1, scalar2=s2, op0=op0, op1=op1),
                       reads, writes)

    def stt(self, out, in0, scalar, in1, op0, op1, reads, writes, eng="dve"):
        self.fw.op(eng, lambda h: h.scalar_tensor_tensor(out=out, in0=in0, scalar=scalar, in1=in1, op0=op0, op1=op1),
                   reads, writes)

    def cp(self, out, in_, reads, writes, eng="dve"):
        if eng == "act":
            self.act(out, in_, AF.Copy, reads, writes)
        else:
            self.fw.op(eng, lambda h: h.tensor_copy(out=out, in_=in_), reads, writes)

    def recip(self, out, in_, reads, writes):
        self.fw.op("dve", lambda h: h.reciprocal(out=out, in_=in_), reads, writes)

    def memset(self, ap, val, writes, eng="pool"):
        self.fw.op(eng, lambda h: h.memset(ap, val), (), writes)

    def ld(self, out, in_, reads, writes, eng="sp", nodeps=False, nowaw=False):
        return self.fw.dma(eng, lambda h: h.dma_start(out=out, in_=in_), reads, writes, nodeps=nodeps, nowaw=nowaw)

    def evac(self, out, in_, reads, writes):
        self.ev_ctr += 1
        self.cp(out, in_, reads, writes, eng=("act" if self.ev_ctr % 2 else "dve"))

    def cb(self, c0, n=128):
        return self.cst[:, c0:c0 + n]

    def prologue(self):
        nc = self.nc
        self.ld(self.cst[:, :], self.cstb_d, (), [self.t_c])
        self.ld(self.cstf[:, :], self.cstf_d, (), [self.t_c])
        self.ld(self.gains[:, :], self.gains_d, (), [self.t_c])
        self.ld(self.qkg[:, :], self.qkg_d, (), [self.t_c])
        self.ld(self.sinkexp[:, :], self.sinks_d, (), [self.t_c])
        self.act(self.sinkexp[:, :], self.sinkexp[:, :], AF.Exp, [self.t_c], [self.t_c])
        def cast(dst, src, tok, rows, step):
            for r0 in range(0, rows, step):
                self.ld(dst[r0:r0 + step, :], src[r0:r0 + step, :], (), [tok], eng="pool", nodeps=True)
        isb = isw = 0
        for idx, li in enumerate(self.layers):
            cast(self.wgu_b[2 * idx], self.wgu[2 * idx], self.t_wgu_b[2 * idx], D, 128)
            cast(self.wd_b[2 * idx], self.wd[2 * idx], self.t_wd_b[2 * idx], DFF, 704)
            if li % 2 == 0:
                cast(self.sbin_b[isb], self.sbin[isb], self.t_sbin_b[isb], D, 256)
                cast(self.sbout_b[isb], self.sbout[isb], self.t_sbout_b[isb], D, 512)
                isb += 1
            else:
                cast(self.swin_b[isw], self.swin[isw], self.t_swin_b[isw], D, 512)
                cast(self.swout_b[isw], self.swout[isw], self.t_swout_b[isw], D, 512)
                isw += 1
            cast(self.wgu_b[2 * idx + 1], self.wgu[2 * idx + 1], self.t_wgu_b[2 * idx + 1], D, 128)
            cast(self.wd_b[2 * idx + 1], self.wd[2 * idx + 1], self.t_wd_b[2 * idx + 1], DFF, 704)
        self.carve_reset()
        pint = [self.arena[:, i * 512:(i + 1) * 512].bitcast(I32) for i in range(2)]
        self.aoff = 1024
        pf = [self.carve([128, 512], F32) for _ in range(2)]
        ang = [self.carve([128, 512], F32) for _ in range(2)]
        kf = [self.carve([128, 512], F32) for _ in range(2)]
        ki = [self.carve([128, 512], F32).bitcast(I32) for _ in range(2)]
        gt = [self.carve([128, 512], F32) for _ in range(2)]
        res = [self.carve([128, 512], F32) for _ in range(4)]
        t_pi = [Tok() for _ in range(2)]
        t_pf = [Tok() for _ in range(2)]
        t_ang = [Tok() for _ in range(2)]
        t_kf = [Tok() for _ in range(2)]
        t_ki = [Tok() for _ in range(2)]
        t_gt = [Tok() for _ in range(2)]
        t_res = [Tok() for _ in range(4)]
        invf = self.cstf[:, 0:1]
        for j in range(NT):
            s = j % 2
            self.ld(pint[s], self.pos[:, j * 512:(j + 1) * 512].broadcast_to([128, 512]), (), [t_pi[s]])
            self.cp(pf[s], pint[s], [t_pi[s]], [t_pf[s]])
            for which in range(2):
                r = (2 * j + which) % 4
                if which == 0:
                    self.ts(ang[s], pf[s], invf, ALU.mult, [t_pf[s], self.t_c], [t_ang[s]])
                else:
                    self.ts(ang[s], pf[s], invf, ALU.mult, [t_pf[s], self.t_c], [t_ang[s]], s2=PI / 2, op1=ALU.add)
                self.ts(kf[s], ang[s], 1.0 / TWO_PI, ALU.mult, [t_ang[s]], [t_kf[s]])
                self.cp(ki[s], kf[s], [t_kf[s]], [t_ki[s]])
                self.cp(kf[s], ki[s], [t_ki[s]], [t_kf[s]])
                self.stt(ang[s], kf[s], -TWO_PI, ang[s], ALU.mult, ALU.add, [t_kf[s], t_ang[s]], [t_ang[s]])
                self.ts(gt[s], ang[s], PI, ALU.is_gt, [t_ang[s]], [t_gt[s]], s2=-TWO_PI, op1=ALU.mult)
                self.tt(ang[s], ang[s], gt[s], ALU.add, [t_ang[s], t_gt[s]], [t_ang[s]])
                self.ts(ang[s], ang[s], -PI, ALU.max, [t_ang[s]], [t_ang[s]], s2=PI, op1=ALU.min)
                self.act(res[r], ang[s], AF.Sin, [t_ang[s]], [t_res[r]])
                dst = self.sin_d if which == 0 else self.cos_d
                self.ld(dst[:, j * 512:(j + 1) * 512], res[r], [t_res[r]], [self.t_cs], nowaw=True)
        self.fw.barrier()

    def X3(self, ap):
        return ap.rearrange("(kc p) t -> p kc t", p=128)

    def load_x(self, src, j, slot):
        self.ld(self.xt[slot][:, :, :], self.X3(src)[:, :, j * 512:(j + 1) * 512],
                [self.t_X[j]], self.t_xt[slot])

    def store_x(self, j, slot):
        self.ld(self.X3(self.yT)[:, :, j * 512:(j + 1) * 512], self.xt[slot][:, :, :],
                self.t_xt[slot], [self.t_X[j]], nowaw=True)

    def norm(self, slot, gidx, bank=7):
        xt = self.xt[slot]
        ps = self.PS[bank]
        tps = self.t_ps[bank]
        n = self.norm_ctr
        self.norm_ctr += 1
        for kc in range(8):
            q = (n * 8 + kc) % 4
            self.act(self.sq[q][:, :], xt[:, kc, :], AF.Square, [self.t_xt[slot][kc]], [self.t_sq[q]])
            self.mm(ps[:, :], self.cb(C_ONE), self.sq[q][:, :], kc == 0, kc == 7, [self.t_sq[q], self.t_c], [tps])
        r = n % 2
        self.act(self.rt[r][:, :], ps[:, :], AF.Ln, [tps], [self.t_rt[r]], scale=1.0 / D, bias=EPS)
        self.act(self.rstd[r][:, :], self.rt[r][:, :], AF.Exp, [self.t_rt[r]], [self.t_rstd[r]], scale=-0.5)
        for kc in range(8):
            self.stt(self.h[slot][:, kc, :], xt[:, kc, :], self.gains[:, gidx * 8 + kc:gidx * 8 + kc + 1],
                     self.rstd[r][:, :], ALU.mult, ALU.mult,
                     [self.t_xt[slot][kc], self.t_rstd[r], self.t_c], [self.t_h[slot][kc]])

    def end_pass(self):
        self.fw.barrier()
        self.pass_count += 1
        return self.n_passes is not None and self.pass_count >= self.n_passes

    def ffn_pass(self, f, gidx, src):
        self.carve_reset()
        actT = [self.carve([128, 22, 512], BF16) for _ in range(2)]
        t_act = [[self.T("p1_") for _ in range(22)] for _ in range(2)]
        tmp = [self.carve([128, 512], F32) for _ in range(2)]
        t_tmp = [self.T("p2_") for _ in range(2)]
        wg = [self.carve([128, 8, 256], BF16) for _ in range(4)]
        wu = [self.carve([128, 8, 256], BF16) for _ in range(4)]
        t_wg = [self.T("p3_") for _ in range(4)]
        t_wu = [self.T("p4_") for _ in range(4)]
        wdn = [self.carve([128, 22, 256], BF16) for _ in range(3)]
        t_wdn = [self.T("p5_") for _ in range(3)]
        wgu3 = self.wgu_b[f].rearrange("(kc p) n -> p kc n", p=128)
        wd3 = self.wd_b[f].rearrange("(c p) n -> p c n", p=128)
        PS, tps = self.PS, self.t_ps
        wctr = 0
        dctr = 0
        ectr = 0
        for T in range(NT // 2):
            for s in range(2):
                self.load_x(src, 2 * T + s, s)
                self.norm(s, gidx)
            for g in range(11):
                ws = wctr % 4
                wctr += 1
                self.ld(wg[ws], wgu3[:, :, g * 256:(g + 1) * 256], [self.t_wgu_b[f]], [t_wg[ws]])
                self.ld(wu[ws], wgu3[:, :, DFF + g * 256:DFF + (g + 1) * 256], [self.t_wgu_b[f]], [t_wu[ws]])
                for s in range(2):
                    for cc in range(2):
                        c = 2 * g + cc
                        e = ectr % 2
                        ectr += 1
                        bg, bu = e, 2 + e
                        for kc in range(8):
                            self.mm(PS[bg][:, :], wg[ws][:, kc, cc * 128:(cc + 1) * 128], self.h[s][:, kc, :],
                                    kc == 0, kc == 7, [t_wg[ws], self.t_h[s][kc]], [tps[bg]])
                        for kc in range(8):
                            self.mm(PS[bu][:, :], wu[ws][:, kc, cc * 128:(cc + 1) * 128], self.h[s][:, kc, :],
                                    kc == 0, kc == 7, [t_wu[ws], self.t_h[s][kc]], [tps[bu]])
                        self.act(tmp[e], PS[bg][:, :], AF.Silu, [tps[bg]], [t_tmp[e]])
                        self.tt(actT[s][:, c, :], tmp[e], PS[bu][:, :], ALU.mult, [t_tmp[e], tps[bu]], [t_act[s][c]])
            for dg in range(4):
                ds_ = dctr % 3
                dctr += 1
                self.ld(wdn[ds_], wd3[:, :, dg * 256:(dg + 1) * 256], [self.t_wd_b[f]], [t_wdn[ds_]])
                for s in range(2):
                    for dd in range(2):
                        dc = 2 * dg + dd
                        e = ectr % 2
                        ectr += 1
                        by = 4 + e
                        for c in range(22):
                            self.mm(PS[by][:, :], wdn[ds_][:, c, dd * 128:(dd + 1) * 128], actT[s][:, c, :],
                                    c == 0, c == 21, [t_wdn[ds_], t_act[s][c]], [tps[by]])
                        self.stt(self.xt[s][:, dc, :], PS[by][:, :], 0.5, self.xt[s][:, dc, :], ALU.mult, ALU.add,
                                 [tps[by], self.t_xt[s][dc]], [self.t_xt[s][dc]])
            for s in range(2):
                self.store_x(2 * T + s, s)
        return self.end_pass()

    def outproj_pass(self, w_b, t_w):
        self.carve_reset()
        wout = self.carve([128, 8, 1024], BF16)
        t_wout = self.T("p6_")
        oTt = [self.carve([128, 8, 512], BF16) for _ in range(2)]
        t_oTt = [self.T("p7_") for _ in range(2)]
        self.ld(wout, w_b.rearrange("(c p) n -> p c n", p=128), [t_w], [t_wout])
        o3 = self.oT_d.rearrange("h p t -> p h t")
        PS, tps = self.PS, self.t_ps
        for j in range(NT):
            s = j % 2
            self.load_x(self.yT, j, s)
            self.ld(oTt[s], o3[:, :, j * 512:(j + 1) * 512], [self.t_oT[0]], [t_oTt[s]])
            for dc in range(8):
                b = dc % 4
                for c in range(8):
                    self.mm(PS[b][:, :], wout[:, c, dc * 128:(dc + 1) * 128], oTt[s][:, c, :], c == 0, c == 7,
                            [t_wout, t_oTt[s]], [tps[b]])
                self.tt(self.xt[s][:, dc, :], PS[b][:, :], self.xt[s][:, dc, :], ALU.add,
                        [tps[b], self.t_xt[s][dc]], [self.t_xt[s][dc]])
            self.store_x(j, s)
        return self.end_pass()

    def sb_qkv_pass(self, sl, gidx):
        self.carve_reset()
        w = [self.carve([128, 8, 512], BF16) for _ in range(2)]
        t_w = [self.T("p8_") for _ in range(2)]
        st = [self.carve([128, 4, 512], BF16) for _ in range(2)]
        t_st = [[self.T("p9_") for _ in range(4)] for _ in range(2)]
        win3 = self.sbin_b[sl].rearrange("(kc p) n -> p kc n", p=128)
        q3 = self.qT_d.rearrange("h p t -> p h t")
        k3 = self.kT_d.rearrange("h p t -> p h t")
        v3 = self.v_d.rearrange("(b s) f -> s b f", s=128)
        PS, tps = self.PS, self.t_ps
        wctr = 0
        bctr = 0
        for j in range(NT):
            s = j % 2
            self.load_x(self.yT, j, s)
            self.norm(s, gidx)
            for grp in range(6):
                ws = wctr % 2
                wctr += 1
                self.ld(w[ws], win3[:, :, grp * 512:(grp + 1) * 512], [self.t_sbin_b[sl]], [t_w[ws]])
                ss = ws
                for i4 in range(4):
                    b = bctr % 4
                    bctr += 1
                    if grp < 4:
                        for kc in range(8):
                            self.mm(PS[b][:, :], w[ws][:, kc, i4 * 128:(i4 + 1) * 128], self.h[s][:, kc, :],
                                    kc == 0, kc == 7, [t_w[ws], self.t_h[s][kc]], [tps[b]])
                    else:
                        for kc in range(8):
                            self.mm(PS[b][:, :], self.h[s][:, kc, i4 * 128:(i4 + 1) * 128], w[ws][:, kc, :],
                                    kc == 0, kc == 7, [t_w[ws], self.t_h[s][kc]], [tps[b]])
                    self.evac(st[ss][:, i4, :], PS[b][:, :], [tps[b]], [t_st[ss][i4]])
                tsl = slice(j * 512, (j + 1) * 512)
                if grp < 2:
                    self.ld(q3[:, grp * 4:(grp + 1) * 4, tsl], st[ss], t_st[ss], [self.t_qT[0]], nowaw=True)
                elif grp < 4:
                    g2 = grp - 2
                    self.ld(k3[:, g2 * 4:(g2 + 1) * 4, tsl], st[ss], t_st[ss], [self.t_kT[0]], nowaw=True)
                else:
                    g2 = grp - 4
                    self.ld(v3[:, j * 4:(j + 1) * 4, g2 * 512:(g2 + 1) * 512], st[ss], t_st[ss], [self.t_v], nowaw=True)
        return self.end_pass()

    def sb_attn_pass(self):
        self.carve_reset()
        qT = self.carve([128, S], BF16)
        kT = self.carve([128, S], BF16)
        vp2 = self.carve([128, 32 * 2 * 128], BF16)
        vp = vp2.rearrange("p (a b c) -> p a b c", a=32, b=2)
        oTh = self.carve([128, S], BF16)
        NS = 4
        et = [self.carve([128, 512], F32) for _ in range(NS)]
        sp = [self.carve([128, 512], BF16) for _ in range(NS)]
        arg = [self.carve([128, 512], F32) for _ in range(NS)]
        Am = [self.carve([128, 512], BF16) for _ in range(NS)]
        Rb = [self.carve([128, 512], F32) for _ in range(2)]
        t_q, t_k, t_vp = self.T("p10_"), self.T("p11_"), self.T("p12_")
        t_o = [self.T("p13_") for _ in range(8)]
        t_et = [self.T("p14_") for _ in range(NS)]
        t_sp = [self.T("p15_") for _ in range(NS)]
        t_arg = [self.T("p16_") for _ in range(NS)]
        t_A = [self.T("p17_") for _ in range(NS)]
        t_R = [self.T("p18_") for _ in range(2)]
        PS, tps = self.PS, self.t_ps
        ident = self.cb(C_ID)
        tri = self.cb(C_TRI)
        neg1 = self.cb(C_NEG1)
        v3 = self.v_d.rearrange("(b s) f -> s b f", s=128)
        self.memset(vp2, 0.0, [t_vp])
        tc_ = self.t_c

        for hp in range(8):
            self.ld(qT, self.qT_d[hp], [self.t_qT[hp]], [t_q])
            self.ld(kT, self.kT_d[hp], [self.t_kT[hp]], [t_k])
            for e in range(2):
                for b0 in range(0, 32, 8):
                    self.ld(vp[:, b0:b0 + 8, e, e * 64:(e + 1) * 64],
                            v3[:, b0:b0 + 8, (2 * hp + e) * 64:(2 * hp + e + 1) * 64], [self.t_v], [t_vp])
            units = []
            for qg in range(8):
                for e in range(2):
                    kmax = 4 * qg + 3
                    for kb in range(kmax, -1, -1):
                        units.append((qg, e, kb))
            NU = len(units)

            NZ = 4

            def c0_of(u):
                qg, e, kb = units[u]
                m = kb - 4 * qg
                return 128 * m if m > 0 else 0

            def s1(u):
                qg, e, kb = units[u]
                rows = slice(e * 64, (e + 1) * 64)
                zb = u % NZ
                diag = kb >= 4 * qg
                c0 = c0_of(u)
                self.mm(PS[zb][:, c0:512], kT[rows, kb * 128:(kb + 1) * 128], qT[rows, qg * 512 + c0:(qg + 1) * 512],
                        True, False, [t_k, t_q], [tps[zb]], skip=True)
                if diag:
                    m = kb - 4 * qg
                    self.mm(PS[zb][:, c0:512], ident, self.cst[:, C_SBM + m * 512 + c0:C_SBM + (m + 1) * 512],
                            False, False, [tc_], [tps[zb]], skip=True)

            def s2a(u):
                zb = u % NZ
                sl_ = u % NS
                c0 = c0_of(u)
                self.act(et[sl_][:, c0:512], PS[zb][:, c0:512], AF.Exp, [tps[zb]], [t_et[sl_]], scale=0.125)

            def s2b(u):
                sl_ = u % NS
                c0 = c0_of(u)
                self.act(sp[sl_][:, c0:512], et[sl_][:, c0:512], AF.Ln, [t_et[sl_]], [t_sp[sl_]], bias=1.0)

            def s3(u):
                qg, e, kb = units[u]
                zb = u % NZ
                rbk = 4 + u % 2
                sl_ = u % NS
                c0 = c0_of(u)
                self.mm(PS[zb][:, c0:512], tri, sp[sl_][:, c0:512], False, True, [tc_, t_sp[sl_]], [tps[zb]], skip=True)
                if kb > 0:
                    self.mm(PS[rbk][:, c0:512], neg1, sp[sl_][:, c0:512], True, True, [tc_, t_sp[sl_]], [tps[rbk]])

            def s4(u):
                qg, e, kb = units[u]
                zb = u % NZ
                rbk = 4 + u % 2
                sl_ = u % NS
                gi = (qg * 2 + e) % 2
                first = kb == 4 * qg + 3
                c0 = c0_of(u)
                if first:
                    self.memset(Rb[gi][:, 0:c0], 0.0, [t_R[gi]], eng="dve")
                    self.ts(PS[zb][:, c0:512], PS[zb][:, c0:512], 0.125, ALU.mult, [tps[zb]], [tps[zb]])
                    self.cp(Rb[gi][:, c0:512], PS[rbk][:, c0:512], [tps[rbk]], [t_R[gi]])
                else:
                    self.stt(PS[zb][:, c0:512], PS[zb][:, c0:512], 0.125, Rb[gi][:, c0:512], ALU.mult, ALU.add,
                             [tps[zb], t_R[gi]], [tps[zb]])
                    if kb > 0:
                        self.tt(Rb[gi][:, c0:512], Rb[gi][:, c0:512], PS[rbk][:, c0:512], ALU.add,
                                [t_R[gi], tps[rbk]], [t_R[gi]])

            def s5(u):
                sl_ = u % NS
                c0 = c0_of(u)
                zb = u % NZ
                self.act(Am[sl_][:, c0:512], PS[zb][:, c0:512], AF.Exp, [tps[zb]], [t_A[sl_]])

            def s6(u):
                qg, e, kb = units[u]
                sl_ = u % NS
                ob = 6 + qg % 2
                c0 = c0_of(u)
                first = (e == 0 and kb == 4 * qg + 3)
                last = (e == 1 and kb == 0)
                self.mm(PS[ob][:, c0:512], vp[:, kb, e, :], Am[sl_][:, c0:512], first, last, [t_vp, t_A[sl_]], [tps[ob]],
                        skip=True)
                if last:
                    self.evac(oTh[:, qg * 512:(qg + 1) * 512], PS[ob][:, :], [tps[ob]], [t_o[qg]])

            s1(0)
            for tau in range(NU + 3):
                if tau + 1 < NU:
                    s1(tau + 1)
                if tau < NU:
                    s2a(tau)
                if 0 <= tau - 2 < NU:
                    s5(tau - 2)
                if tau < NU:
                    s2b(tau)
                if 0 <= tau - 1 < NU:
                    s3(tau - 1)
                    s4(tau - 1)
                if 0 <= tau - 3 < NU:
                    s6(tau - 3)
            self.ld(self.oT_d[hp], oTh, t_o, [self.t_oT[hp]], nowaw=True)
        return self.end_pass()

    def sw_qkv_pass(self, sl, gidx, gsl):
        self.carve_reset()
        NSL = 3
        win = self.carve([128, 8, 1280], BF16)
        kdw = self.carve([128, 8, 256], BF16)
        t_win, t_kdw = self.T("w"), self.T("w")
        cs = [self.carve([128, 512], F32) for _ in range(2)]
        sn = [self.carve([128, 512], F32) for _ in range(2)]
        t_cs = [self.T("w") for _ in range(2)]
        t_sn = [self.T("w") for _ in range(2)]
        qn = [self.carve([128, 512], F32) for _ in range(NSL)]
        qnb = [self.carve([128, 512], BF16) for _ in range(NSL)]
        sqq = [self.carve([128, 512], BF16) for _ in range(NSL)]
        rq = [self.carve([128, 512], F32) for _ in range(NSL)]
        t1 = [self.carve([128, 512], F32) for _ in range(NSL)]
        t2 = [self.carve([128, 512], F32) for _ in range(NSL)]
        t_qn = [self.T("w") for _ in range(NSL)]
        t_qnb = [self.T("w") for _ in range(NSL)]
        t_sqq = [self.T("w") for _ in range(NSL)]
        t_rq = [self.T("w") for _ in range(NSL)]
        t_t1 = [self.T("w") for _ in range(NSL)]
        t_t2 = [self.T("w") for _ in range(NSL)]
        st = [self.carve([128, 4, 512], BF16) for _ in range(3)]
        t_st = [[self.T("w") for _ in range(4)] for _ in range(3)]
        vst = [self.carve([128, 4, 128], BF16) for _ in range(2)]
        t_vst = [self.T("w") for _ in range(2)]
        win3 = self.swin_b[sl].rearrange("(kc p) n -> p kc n", p=128)
        self.ld(win, win3, [self.t_swin_b[sl]], [t_win])
        for g in range(2):
            for dup in range(2):
                c0 = (g * 2 + dup) * 64
                self.ld(kdw[:, :, c0:c0 + 64], win3[:, :, 1024 + g * 64:1024 + (g + 1) * 64],
                        [self.t_swin_b[sl]], [t_kdw])
        q3 = self.qT_d.rearrange("h p t -> p h t")
        k3 = self.kT_d.rearrange("h p t -> p h t")
        v3 = self.v2_d.rearrange("(b s) f -> s b f", s=128)
        PS, tps = self.PS, self.t_ps
        blk = self.cb(C_BLK)
        swp = self.cb(C_SWP)
        tc_ = self.t_c
        units = [(j, u) for j in range(NT) for u in range(10)]
        NU = len(units)

        def prep_tile(j):
            s = j % 2
            tsl = slice(j * 512, (j + 1) * 512)
            self.load_x(self.yT, j, s)
            self.norm(s, gidx)
            self.ld(cs[s], self.cos_d[:, tsl], [self.t_cs], [t_cs[s]])
            self.ld(sn[s], self.sin_d[:, tsl], [self.t_cs], [t_sn[s]])

        def do_v(j):
            s = j % 2
            for i4 in range(4):
                for kc in range(8):
                    self.mm(PS[7][:, i4 * 128:(i4 + 1) * 128], self.h[s][:, kc, i4 * 128:(i4 + 1) * 128],
                            win[:, kc, 1152:1280], kc == 0, kc == 7, [t_win, self.t_h[s][kc]], [tps[7]])
            self.evac(vst[s].rearrange("p a b -> p (a b)"), PS[7][:, :], [tps[7]], [t_vst[s]])
            self.ld(v3[:, j * 4:(j + 1) * 4, :], vst[s], [t_vst[s]], [self.t_v], nowaw=True)

        def p1(i):
            j, unit = units[i]
            s = j % 2
            if unit == 0:
                prep_tile(j)
            u = i % NSL
            b0 = u
            isq = unit < 8
            for kc in range(8):
                lhs = win[:, kc, unit * 128:(unit + 1) * 128] if isq else kdw[:, kc, (unit - 8) * 128:(unit - 7) * 128]
                self.mm(PS[b0][:, :], lhs, self.h[s][:, kc, :], kc == 0, kc == 7,
                        [t_win if isq else t_kdw, self.t_h[s][kc]], [tps[b0]])
            self.act(sqq[u], PS[b0][:, :], AF.Square, [tps[b0]], [t_sqq[u]])
            if unit == 9:
                do_v(j)

        def p2(i):
            j, unit = units[i]
            u = i % NSL
            b0 = u
            b1 = 3 + i % 2
            isq = unit < 8
            gc = 2 * gsl + (0 if isq else 1)
            gcol = self.qkg[:, gc:gc + 1]
            self.mm(PS[b1][:, :], blk, sqq[u], True, True, [tc_, t_sqq[u]], [tps[b1]])
            self.act(rq[u], PS[b1][:, :], AF.Ln, [tps[b1]], [t_rq[u]], scale=1.0 / 64, bias=EPS)
            self.act(rq[u], rq[u], AF.Exp, [t_rq[u]], [t_rq[u]], scale=-0.5)
            self.stt(qn[u], PS[b0][:, :], gcol, rq[u], ALU.mult, ALU.mult, [tps[b0], t_rq[u], tc_], [t_qn[u]])
            self.cp(qnb[u], qn[u], [t_qn[u]], [t_qnb[u]], eng="act")

        def p3(i):
            j, unit = units[i]
            s = j % 2
            tsl = slice(j * 512, (j + 1) * 512)
            u = i % NSL
            b2 = 5 + i % 2
            isq = unit < 8
            self.mm(PS[b2][:, :], swp, qnb[u], True, True, [tc_, t_qnb[u]], [tps[b2]])
            self.tt(t1[u], qn[u], cs[s], ALU.mult, [t_qn[u], t_cs[s]], [t_t1[u]])
            self.tt(t2[u], PS[b2][:, :], sn[s], ALU.mult, [tps[b2], t_sn[s]], [t_t2[u]])
            if isq:
                ss = (unit // 4 + 2 * j) % 2
                i4 = unit % 4
                self.tt(st[ss][:, i4, :], t1[u], t2[u], ALU.add, [t_t1[u], t_t2[u]], [t_st[ss][i4]])
                if i4 == 3:
                    g4 = unit // 4
                    self.ld(q3[:, g4 * 4:(g4 + 1) * 4, tsl], st[ss], t_st[ss], [self.t_qT[0]], nowaw=True)
            else:
                ss = 2
                i4 = unit - 8
                self.tt(st[ss][:, i4, :], t1[u], t2[u], ALU.add, [t_t1[u], t_t2[u]], [t_st[ss][i4]])
                if i4 == 1:
                    self.ld(k3[:, 0:2, tsl], st[ss][:, 0:2, :], t_st[ss][0:2], [self.t_kT[0]], nowaw=True)

        for tau in range(NU + 2):
            if tau < NU:
                p1(tau)
            if 0 <= tau - 1 < NU:
                p2(tau - 1)
            if 0 <= tau - 2 < NU:
                p3(tau - 2)
        return self.end_pass()

    def sw_attn_pass(self, sl):
        self.carve_reset()
        qT = [self.carve([128, S], BF16) for _ in range(2)]
        t_q = [self.T("w") for _ in range(2)]
        kTd = self.carve([128, 2, S], BF16)
        t_k = self.T("w")
        vpad2 = [[self.carve([128, 32 * 128], BF16) for _ in range(2)] for _ in range(2)]
        vpad = [[vpad2[g][e].rearrange("p (a b) -> p a b", a=32) for e in range(2)] for g in range(2)]
        t_vp = self.T("w")
        oTh = [self.carve([128, S], BF16) for _ in range(2)]
        t_o = [[self.T("w") for _ in range(8)] for _ in range(2)]
        NA = 4
        Am = [self.carve([128, 512], BF16) for _ in range(NA)]
        t_A = [self.T("w") for _ in range(NA)]
        den = [self.carve([128, 512], F32) for _ in range(2)]
        t_den = [self.T("w") for _ in range(2)]
        PS, tps = self.PS, self.t_ps
        SBK = [0, 1, 6, 7]
        ident = self.cb(C_ID)
        tc_ = self.t_c
        swm = self.cst[:, C_SWM:C_SWM + 512]
        onesE = [self.cb(C_E0), self.cb(C_E1)]
        v3 = self.v2_d.rearrange("(b s) f -> s b f", s=128)
        k3 = self.kT_d.rearrange("h p t -> p h t")
        for g in range(2):
            for e in range(2):
                self.memset(vpad2[g][e], 0.0, [t_vp])
        self.ld(kTd, k3[:, 0:2, :], [self.t_kT[0]], [t_k])
        for g in range(2):
            for e in range(2):
                for b0 in range(0, 32, 8):
                    self.ld(vpad[g][e][:, b0:b0 + 8, e * 64:(e + 1) * 64], v3[:, b0:b0 + 8, g * 64:(g + 1) * 64],
                            [self.t_v], [t_vp])
        units = [(hp, kb, e) for hp in range(8) for kb in range(32) for e in range(2)]
        NU = len(units)

        def s1(i):
            hp, kb, e = units[i]
            qs = hp % 2
            g = hp // 4
            if kb == 0 and e == 0:
                self.ld(qT[qs], self.qT_d[hp], [self.t_qT[hp]], [t_q[qs]])
            bk = SBK[i % 4]
            a_ = i % NA
            n = 256 if kb < 31 else 128
            rows = slice(e * 64, (e + 1) * 64)
            self.mm(PS[bk][:, 0:n], kTd[rows, g, kb * 128:(kb + 1) * 128], qT[qs][rows, kb * 128:kb * 128 + n],
                    True, False, [t_k, t_q[qs]], [tps[bk]])
            self.mm(PS[bk][:, 0:n], ident, swm[:, 0:n], False, True, [tc_], [tps[bk]])
            self.act(Am[a_][:, 0:n], PS[bk][:, 0:n], AF.Exp, [tps[bk]], [t_A[a_]], scale=0.125)

        def s3(i):
            hp, kb, e = units[i]
            qs = hp % 2
            g = hp // 4
            a_ = i % NA
            for part in range(2):
                qb = kb + part
                if qb > 31:
                    continue
                qg = qb // 4
                ob = 2 + qg % 2
                db = 4 + qg % 2
                cols = slice((qb % 4) * 128, (qb % 4 + 1) * 128)
                first = (part == 1 and e == 0 and qb % 4 == 0) or (qb == 0 and part == 0 and e == 0)
                last = (part == 0 and e == 1)
                asl = Am[a_][:, part * 128:(part + 1) * 128]
                self.mm(PS[ob][:, cols], vpad[g][e][:, kb, :], asl, first, last, [t_vp, t_A[a_]], [tps[ob]], skip=True)
                self.mm(PS[db][:, cols], onesE[e], asl, first, last, [tc_, t_A[a_]], [tps[db]], skip=True)
            if e == 1 and kb % 4 == 3:
                qg = kb // 4
                ob = 2 + qg % 2
                db = 4 + qg % 2
                d_ = qg % 2
                self.ts(den[d_], PS[db][:, :], self.sinkexp[:, sl * 8 + hp:sl * 8 + hp + 1], ALU.add,
                        [tps[db], tc_], [t_den[d_]])
                self.recip(den[d_], den[d_], [t_den[d_]], [t_den[d_]])
                self.tt(oTh[qs][:, qg * 512:(qg + 1) * 512], PS[ob][:, :], den[d_], ALU.mult,
                        [tps[ob], t_den[d_]], [t_o[qs][qg]])
            if e == 1 and kb == 31:
                self.ld(self.oT_d[hp], oTh[qs], t_o[qs], [self.t_oT[hp]], nowaw=True)

        for tau in range(NU + 2):
            if tau < NU:
                s1(tau)
            if 0 <= tau - 2 < NU:
                s3(tau - 2)
        return self.end_pass()

    def build(self):
        self.prologue()
        isb = isw = 0
        for idx, li in enumerate(self.layers):
            gsl = li // 2
            src = self.xT if idx == 0 else self.yT
            if self.ffn_pass(2 * idx, li * 3 + 0, src):
                break
            if li % 2 == 0:
                if self.sb_qkv_pass(isb, li * 3 + 1):
                    break
                if self.sb_attn_pass():
                    break
                if self.outproj_pass(self.sbout_b[isb], self.t_sbout_b[isb]):
                    break
                isb += 1
            else:
                if self.sw_qkv_pass(isw, li * 3 + 1, gsl):
                    break
                if self.sw_attn_pass(gsl):
                    break
                if self.outproj_pass(self.swout_b[isw], self.t_swout_b[isw]):
                    break
                isw += 1
            if self.ffn_pass(2 * idx + 1, li * 3 + 2, self.yT):
                break
        self.fw.barrier()
        self.fw.finalize()
        return self.nc


def make_consts():
    bf = ml_dtypes.bfloat16
    c = np.zeros((128, NCB), np.float32)
    idx = np.arange(128)
    c[:, C_ID:C_ID + 128] = np.eye(128)
    c[:, C_TRI:C_TRI + 128] = np.where(idx[:, None] >= idx[None, :], -8.0, 0.0)
    c[:, C_NEG1:C_NEG1 + 128] = -1.0
    c[:, C_ONE:C_ONE + 128] = 1.0
    c[:, C_BLK:C_BLK + 128] = (idx[:, None] // 64 == idx[None, :] // 64).astype(np.float32)
    swp = np.zeros((128, 128), np.float32)
    for m in range(128):
        d = m % 64
        if d < 8:
            swp[m + 8, m] = -1.0
        elif d < 16:
            swp[m - 8, m] = 1.0
    c[:, C_SWP:C_SWP + 128] = swp
    c[:, C_E0:C_E0 + 128] = (idx[None, :] < 64).astype(np.float32)
    c[:, C_E1:C_E1 + 128] = (idx[None, :] >= 64).astype(np.float32)
    t = np.arange(512)
    for m in range(4):
        allowed = (m * 128 + idx[:, None]) < t[None, :]
        c[:, C_SBM + m * 512:C_SBM + (m + 1) * 512] = np.where(allowed, 0.0, NEG)
    tl = np.arange(128)
    cur = idx[:, None] <= tl[None, :]
    prev = idx[:, None] > tl[None, :]
    for rep in range(2):
        c[:, C_SWM + rep * 256:C_SWM + rep * 256 + 128] = np.where(cur, 0.0, NEG)
        c[:, C_SWM + rep * 256 + 128:C_SWM + rep * 256 + 256] = np.where(prev, 0.0, NEG)
    cf = np.zeros((128, 4), np.float32)
    inv_freq = (500000.0 ** (-np.arange(0, 16, 2, dtype=np.float32) / 16)).astype(np.float32)
    for p in range(128):
        d = p % 64
        if d < 16:
            cf[p, 0] = inv_freq[d % 8]
    return c.astype(bf), cf


_NC_CACHE = {}

PLAN = [[0, 1, 2, 3]]


def get_nc(layers, n_passes=None):
    key = (tuple(layers), n_passes)
    if key not in _NC_CACHE:
        _NC_CACHE[key] = Builder(n_passes, layers).build()
    return _NC_CACHE[key]


def make_in_maps(inputs, xTs, cores, layers):
    pos = np.asarray(inputs["positions"]).astype(np.int32)
    cstb, cstf = make_consts()
    ng = np.asarray(inputs["norm_gains"], np.float32)
    gains = np.ascontiguousarray(ng.reshape(12, 8, 128).transpose(2, 0, 1).reshape(128, 96))
    qg = np.asarray(inputs["sw_q_gain"], np.float32)
    kg = np.asarray(inputs["sw_k_gain"], np.float32)
    qkg = np.stack([np.tile(qg[0], 2), np.tile(kg[0], 2), np.tile(qg[1], 2), np.tile(kg[1], 2)], axis=1)
    sk = np.asarray(inputs["sw_sinks"], np.float32)
    sinks = np.zeros((128, 16), np.float32)
    for sl in range(2):
        for hp in range(8):
            sinks[:64, sl * 8 + hp] = sk[sl, 2 * hp]
            sinks[64:, sl * 8 + hp] = sk[sl, 2 * hp + 1]
    wgu = np.asarray(inputs["ffn_w_gate_up"], np.float32)
    wd = np.asarray(inputs["ffn_w_down"], np.float32)
    shared = {
        "wgu": np.ascontiguousarray(np.concatenate([wgu[l] for l in layers], axis=0)),
        "wd": np.ascontiguousarray(np.concatenate([wd[l] for l in layers], axis=0)),
        "gains": gains, "qkg": np.ascontiguousarray(qkg), "sinks": sinks, "cstb": cstb, "cstf": cstf,
    }
    sbl = [l // 2 for l in layers if l % 2 == 0]
    swl = [l // 2 for l in layers if l % 2 == 1]
    if sbl:
        shared["sbin"] = np.ascontiguousarray(np.asarray(inputs["sb_w_in"], np.float32)[sbl])
        shared["sbout"] = np.ascontiguousarray(np.asarray(inputs["sb_w_out"], np.float32)[sbl])
    if swl:
        shared["swin"] = np.ascontiguousarray(np.asarray(inputs["sw_w_in"], np.float32)[swl])
        shared["swout"] = np.ascontiguousarray(np.asarray(inputs["sw_w_out"], np.float32)[swl])
    maps = []
    for i, b in enumerate(cores):
        m = dict(shared)
        m["xT"] = xTs[i]
        m["pos"] = np.ascontiguousarray(pos[b].reshape(1, S))
        maps.append(m)
    return maps


def kernel(x, positions, norm_gains, ffn_w_gate_up, ffn_w_down, sb_w_in, sb_w_out,
           sw_w_in, sw_w_out, sw_q_gain, sw_k_gain, sw_sinks):
    inputs = dict(x=x, positions=positions, norm_gains=norm_gains, ffn_w_gate_up=ffn_w_gate_up,
                  ffn_w_down=ffn_w_down, sb_w_in=sb_w_in, sb_w_out=sb_w_out, sw_w_in=sw_w_in,
                  sw_w_out=sw_w_out, sw_q_gain=sw_q_gain, sw_k_gain=sw_k_gain, sw_sinks=sw_sinks)
    x = np.asarray(x)
    cores = list(range(8))
    xTs = [np.ascontiguousarray(x[b].T) for b in cores]
    for layers in PLAN:
        nc = get_nc(layers)
        maps = make_in_maps(inputs, xTs, cores, layers)
        res = run_bass_kernel_spmd(nc, maps, core_ids=cores)
        xTs = [np.asarray(res.results[b]["yT"]) for b in cores]
    out = np.empty((8, S, D), np.float32)
    for b in cores:
        out[b] = xTs[b].T
    return out
```

```python
import numpy as np
import ml_dtypes
import concourse.bass as bass
import concourse.mybir as mybir
from concourse.bass_utils import run_bass_kernel_spmd

F32 = mybir.dt.float32
BF16 = mybir.dt.bfloat16
I32 = mybir.dt.int32
AF = mybir.ActivationFunctionType
ALU = mybir.AluOpType

S = 4096
D = 1024
DFF = 2816
NL = 4
NT = 8
EPS = 1e-6
NEG = -4096.0
PI = float(np.pi)
TWO_PI = float(2 * np.pi)

C_ID, C_TRI, C_NEG1, C_ONE, C_BLK, C_SWP, C_E0, C_E1 = [i * 128 for i in range(8)]
C_SBM = 8 * 128
C_SWM = C_SBM + 4 * 512
NCB = C_SWM + 512


class Tok:
    __slots__ = ("name", "w", "r", "sem", "cnt", "bg")

    def __init__(self, name="", bg=False):
        self.name = name
        self.w = None
        self.r = []
        self.sem = None
        self.cnt = 0
        self.bg = bg


class Eng:
    def __init__(self, name):
        self.name = name
        self.ops = []
        self.n = 0
        self.seen = {}
        self.sem = None
        self.needs_inc = set()


class FW:
    def __init__(self, nc):
        self.nc = nc
        self.E = {n: Eng(n) for n in ("pe", "act", "dve", "pool", "sp")}
        for e in self.E.values():
            e.sem = nc.alloc_semaphore("esem_" + e.name)
        self.nsem = 0
        self.dma_toks = {}
        self.t_bar = Tok("barrier")
        self.bar_fn = None

    def _wait(self, eng, ev):
        if ev is None:
            return
        key = ev[1]
        val = ev[2]
        if ev[0] == "e" and key == eng.name and eng.name == "pe":
            return
        if eng.seen.get(key, -1) >= val:
            return
        eng.seen[key] = val
        eng.ops.append(("w", ev))
        if ev[0] == "e":
            self.E[key].needs_inc.add(val)

    def _deps(self, eng, reads, writes):
        for t in reads:
            self._wait(eng, t.w)
        for t in writes:
            self._wait(eng, t.w)
            for ev in t.r:
                self._wait(eng, ev)

    def op(self, engname, fn, reads=(), writes=()):
        eng = self.E[engname]
        self._deps(eng, reads, writes)
        iid = eng.n
        eng.n += 1
        ev = ("e", engname, iid)
        eng.ops.append(("i", fn, iid))
        for t in reads:
            t.r = [e for e in t.r if not (e[0] == "e" and e[1] == engname)] + [ev]
        for t in writes:
            t.w = ev
            t.r = []
        return ev

    def dma(self, engname, fn, reads=(), writes=(), nodeps=False, nowaw=False):
        eng = self.E[engname]
        if nowaw:
            for t in reads:
                self._wait(eng, t.w)
            for t in writes:
                for ev in t.r:
                    self._wait(eng, ev)
        elif not nodeps:
            self._deps(eng, reads, writes)
        tok = writes[0]
        if tok.sem is None:
            tok.sem = self.nc.alloc_semaphore("dsem_%d" % self.nsem)
            self.nsem += 1
        tok.cnt += 16
        ev = ("d", tok, tok.cnt)
        eng.ops.append(("d", fn, tok))
        if not tok.bg:
            self.dma_toks[id(tok)] = tok
        for t in reads:
            t.r = [e for e in t.r if not (e[0] == "d" and e[1] is tok)] + [ev]
        for t in writes:
            t.w = ev
            t.r = []
        return ev

    def barrier(self):
        sp = self.E["sp"]
        for n, e in self.E.items():
            if n != "sp" and e.n > 0:
                self._wait(sp, ("e", n, e.n - 1))
        for tok in self.dma_toks.values():
            self._wait(sp, ("d", tok, tok.cnt))
        self.dma_toks = {}
        ev = self.dma("sp", self.bar_fn, writes=[self.t_bar], nodeps=True)
        self.dma_toks = {}
        for n, e in self.E.items():
            if n != "sp":
                self._wait(e, ev)
        self._wait(sp, ev)

    def finalize(self):
        nc = self.nc
        valmap = {}
        for e in self.E.values():
            c = 0
            for o in e.ops:
                if o[0] == "i" and o[2] in e.needs_inc:
                    c += 1
                    valmap[(e.name, o[2])] = c
        E = self.E

        def replay(e, h):
            for o in e.ops:
                if o[0] == "w":
                    ev = o[1]
                    if ev[0] == "e":
                        h.wait_ge(E[ev[1]].sem, valmap[(ev[1], ev[2])])
                    else:
                        h.wait_ge(ev[1].sem, ev[2])
                elif o[0] == "i":
                    ins = o[1](h)
                    if o[2] in e.needs_inc:
                        ins.then_inc(e.sem, 1)
                else:
                    ins = o[1](h)
                    ins.then_inc(o[2].sem, 16)

        with nc.Block() as block:
            @block.tensor
            def _(h):
                replay(E["pe"], h)

            @block.scalar
            def _(h):
                replay(E["act"], h)

            @block.vector
            def _(h):
                replay(E["dve"], h)

            @block.gpsimd
            def _(h):
                replay(E["pool"], h)

            @block.sync
            def _(h):
                replay(E["sp"], h)


class Builder:
    def __init__(self, n_passes=None, layers=(0, 1, 2, 3)):
        self.n_passes = n_passes
        self.layers = list(layers)
        self.nsb = sum(1 for l in self.layers if l % 2 == 0)
        self.nsw = sum(1 for l in self.layers if l % 2 == 1)
        NLg = len(self.layers)
        self.pass_count = 0
        nc = bass.Bass("TRN2", target_bir_lowering=False)
        self.nc = nc
        fw = FW(nc)
        self.fw = fw

        def din(name, shape, dt):
            return nc.dram_tensor(name, shape, dt, kind="ExternalInput").ap()

        def dint(name, shape, dt):
            return nc.dram_tensor(name, shape, dt, kind="Internal").ap()

        self.xT = din("xT", [D, S], F32)
        self.pos = din("pos", [1, S], I32)
        self.wgu = din("wgu", [2 * NLg, D, 2 * DFF], F32)
        self.wd = din("wd", [2 * NLg, DFF, D], F32)
        if self.nsb:
            self.sbin = din("sbin", [self.nsb, D, 3072], F32)
            self.sbout = din("sbout", [self.nsb, D, D], F32)
        if self.nsw:
            self.swin = din("swin", [self.nsw, D, 1280], F32)
            self.swout = din("swout", [self.nsw, D, D], F32)
        self.gains_d = din("gains", [128, 96], F32)
        self.qkg_d = din("qkg", [128, 4], F32)
        self.sinks_d = din("sinks", [128, 16], F32)
        self.cstb_d = din("cstb", [128, NCB], BF16)
        self.cstf_d = din("cstf", [128, 4], F32)
        self.yT = nc.dram_tensor("yT", [D, S], F32, kind="ExternalOutput").ap()

        self.wgu_b = dint("wgu_b", [2 * NLg, D, 2 * DFF], BF16)
        self.wd_b = dint("wd_b", [2 * NLg, DFF, D], BF16)
        if self.nsb:
            self.sbin_b = dint("sbin_b", [self.nsb, D, 3072], BF16)
            self.sbout_b = dint("sbout_b", [self.nsb, D, D], BF16)
        if self.nsw:
            self.swin_b = dint("swin_b", [self.nsw, D, 1280], BF16)
            self.swout_b = dint("swout_b", [self.nsw, D, D], BF16)
        self.qT_d = dint("qT_d", [8, 128, S], BF16)
        self.kT_d = dint("kT_d", [8, 128, S], BF16)
        self.v_d = dint("v_d", [S, D], BF16)
        self.v2_d = dint("v2_d", [S, 128], BF16)
        self.oT_d = dint("oT_d", [8, 128, S], BF16)
        self.cos_d = dint("cos_d", [128, S], F32)
        self.sin_d = dint("sin_d", [128, S], F32)
        self.bar_d = dint("bar_d", [2, 64], F32)
        fw.bar_fn = lambda h: h.dma_start(out=self.bar_d[0:1, :], in_=self.bar_d[1:2, :])

        tx = Tok("X")
        self.t_X = [tx for j in range(NT)]
        tf = [Tok("ffnw%d" % i, bg=True) for i in range(8)]
        tm = [Tok("mixw%d" % i, bg=True) for i in range(4)]
        self.t_wgu_b = tf
        self.t_wd_b = tf
        self.t_sbin_b = [tm[0], tm[2]]
        self.t_sbout_b = [tm[0], tm[2]]
        self.t_swin_b = [tm[1], tm[3]]
        self.t_swout_b = [tm[1], tm[3]]
        tq, tk, to = Tok("qT"), Tok("kT"), Tok("oT")
        self.t_qT = [tq for i in range(8)]
        self.t_kT = [tk for i in range(8)]
        self.t_v = Tok("v")
        self.t_oT = [to for i in range(8)]
        self.t_cs = Tok("cossin")

        A = nc.alloc_sbuf_tensor
        self.cst = A("cst", [128, NCB], BF16)
        self.cstf = A("cstf_s", [128, 4], F32)
        self.gains = A("gains_s", [128, 96], F32)
        self.qkg = A("qkg_s", [128, 4], F32)
        self.sinkexp = A("sinkexp", [128, 16], F32)
        self.t_c = Tok("consts")
        self.xt = [A("xt%d" % i, [128, 8, 512], F32) for i in range(2)]
        self.t_xt = [[Tok("xt%d_%d" % (i, k)) for k in range(8)] for i in range(2)]
        self.h = [A("h%d" % i, [128, 8, 512], BF16) for i in range(2)]
        self.t_h = [[Tok("h%d_%d" % (i, k)) for k in range(8)] for i in range(2)]
        self.sq = [A("sq%d" % i, [128, 512], BF16) for i in range(4)]
        self.t_sq = [Tok("sq%d" % i) for i in range(4)]
        self.rt = [A("rt%d" % i, [128, 512], F32) for i in range(2)]
        self.t_rt = [Tok("rt%d" % i) for i in range(2)]
        self.rstd = [A("rstd%d" % i, [128, 512], F32) for i in range(2)]
        self.t_rstd = [Tok("rstd%d" % i) for i in range(2)]
        self.ARENA_W = 31000
        self.arena = A("arena", [128, self.ARENA_W], F32)
        self.PS = [nc.alloc_psum_tensor("ps%d" % i, [128, 512], F32) for i in range(8)]
        self.t_ps = [Tok("ps%d" % i) for i in range(8)]
        self.norm_ctr = 0
        self.ev_ctr = 0

    def carve_reset(self):
        self.aoff = 0
        self.tok_ctr = {}

    def T(self, pfx):
        if not hasattr(self, "tok_cache"):
            self.tok_cache = {}
        n = self.tok_ctr.get("all", 0)
        self.tok_ctr["all"] = n + 1
        key = ("all", n)
        if key not in self.tok_cache:
            self.tok_cache[key] = Tok("%s%d" % (pfx, n))
        return self.tok_cache[key]

    def carve(self, shape, dt):
        n = int(np.prod(shape[1:]))
        words = n if dt == F32 else (n + 1) // 2
        assert self.aoff + words <= self.ARENA_W, "arena overflow %d" % (self.aoff + words)
        v = self.arena[:, self.aoff:self.aoff + words]
        self.aoff += words
        if dt != F32:
            v = v.bitcast(dt)
        if len(shape) == 3:
            v = v.rearrange("p (a b) -> p a b", a=shape[1])
        elif len(shape) == 4:
            v = v.rearrange("p (a b c) -> p a b c", a=shape[1], b=shape[2])
        return v

    def mm(self, out, lhsT, rhs, start, stop, reads, writes, skip=False):
        self.fw.op("pe", lambda h: h.matmul(out, lhsT=lhsT, rhs=rhs, start=start, stop=stop,
                                            skip_group_check=skip), reads, writes)

    def act(self, out, in_, func, reads, writes, scale=1.0, bias=0.0):
        self.fw.op("act", lambda h: h.activation(out=out, in_=in_, func=func, bias=bias, scale=scale),
                   reads, writes)

    def tt(self, out, in0, in1, op, reads, writes, eng="dve"):
        self.fw.op(eng, lambda h: h.tensor_tensor(out=out, in0=in0, in1=in1, op=op), reads, writes)

    def ts(self, out, in0, s1, op0, reads, writes, s2=None, op1=None, eng="dve"):
        if op1 is None:
            self.fw.op(eng, lambda h: h.tensor_scalar(out=out, in0=in0, scalar1=s1, scalar2=None, op0=op0),
                       reads, writes)
        else:
            self.fw.op(eng, lambda h: h.tensor_scalar(out=out, in0=in0, scalar1=s1, scalar2=s2, op0=op0, op1=op1),
                       reads, writes)

    def stt(self, out, in0, scalar, in1, op0, op1, reads, writes, eng="dve"):
        self.fw.op(eng, lambda h: h.scalar_tensor_tensor(out=out, in0=in0, scalar=scalar, in1=in1, op0=op0, op1=op1),
                   reads, writes)

    def cp(self, out, in_, reads, writes, eng="dve"):
        if eng == "act":
            self.act(out, in_, AF.Copy, reads, writes)
        else:
            self.fw.op(eng, lambda h: h.tensor_copy(out=out, in_=in_), reads, writes)

    def recip(self, out, in_, reads, writes):
        self.fw.op("dve", lambda h: h.reciprocal(out=out, in_=in_), reads, writes)

    def memset(self, ap, val, writes, eng="pool"):
        self.fw.op(eng, lambda h: h.memset(ap, val), (), writes)

    def ld(self, out, in_, reads, writes, eng="sp", nodeps=False, nowaw=False):
        return self.fw.dma(eng, lambda h: h.dma_start(out=out, in_=in_), reads, writes, nodeps=nodeps, nowaw=nowaw)

    def evac(self, out, in_, reads, writes):
        self.ev_ctr += 1
        self.cp(out, in_, reads, writes, eng=("act" if self.ev_ctr % 2 else "dve"))

    def cb(self, c0, n=128):
        return self.cst[:, c0:c0 + n]

    def prologue(self):
        nc = self.nc
        self.ld(self.cst[:, :], self.cstb_d, (), [self.t_c])
        self.ld(self.cstf[:, :], self.cstf_d, (), [self.t_c])
        self.ld(self.gains[:, :], self.gains_d, (), [self.t_c])
        self.ld(self.qkg[:, :], self.qkg_d, (), [self.t_c])
        self.ld(self.sinkexp[:, :], self.sinks_d, (), [self.t_c])
        self.act(self.sinkexp[:, :], self.sinkexp[:, :], AF.Exp, [self.t_c], [self.t_c])
        def cast(dst, src, tok, rows, step):
            for r0 in range(0, rows, step):
                self.ld(dst[r0:r0 + step, :], src[r0:r0 + step, :], (), [tok], eng="pool", nodeps=True)
        isb = isw = 0
        for idx, li in enumerate(self.layers):
            cast(self.wgu_b[2 * idx], self.wgu[2 * idx], self.t_wgu_b[2 * idx], D, 128)
            cast(self.wd_b[2 * idx], self.wd[2 * idx], self.t_wd_b[2 * idx], DFF, 704)
            if li % 2 == 0:
                cast(self.sbin_b[isb], self.sbin[isb], self.t_sbin_b[isb], D, 256)
                cast(self.sbout_b[isb], self.sbout[isb], self.t_sbout_b[isb], D, 512)
                isb += 1
            else:
                cast(self.swin_b[isw], self.swin[isw], self.t_swin_b[isw], D, 512)
                cast(self.swout_b[isw], self.swout[isw], self.t_swout_b[isw], D, 512)
                isw += 1
            cast(self.wgu_b[2 * idx + 1], self.wgu[2 * idx + 1], self.t_wgu_b[2 * idx + 1], D, 128)
            cast(self.wd_b[2 * idx + 1], self.wd[2 * idx + 1], self.t_wd_b[2 * idx + 1], DFF, 704)
        self.carve_reset()
        pint = [self.arena[:, i * 512:(i + 1) * 512].bitcast(I32) for i in range(2)]
        self.aoff = 1024
        pf = [self.carve([128, 512], F32) for _ in range(2)]
        ang = [self.carve([128, 512], F32) for _ in range(2)]
        kf = [self.carve([128, 512], F32) for _ in range(2)]
        ki = [self.carve([128, 512], F32).bitcast(I32) for _ in range(2)]
        gt = [self.carve([128, 512], F32) for _ in range(2)]
        res = [self.carve([128, 512], F32) for _ in range(4)]
        t_pi = [Tok() for _ in range(2)]
        t_pf = [Tok() for _ in range(2)]
        t_ang = [Tok() for _ in range(2)]
        t_kf = [Tok() for _ in range(2)]
        t_ki = [Tok() for _ in range(2)]
        t_gt = [Tok() for _ in range(2)]
        t_res = [Tok() for _ in range(4)]
        invf = self.cstf[:, 0:1]
        for j in range(NT):
            s = j % 2
            self.ld(pint[s], self.pos[:, j * 512:(j + 1) * 512].broadcast_to([128, 512]), (), [t_pi[s]])
            self.cp(pf[s], pint[s], [t_pi[s]], [t_pf[s]])
            for which in range(2):
                r = (2 * j + which) % 4
                if which == 0:
                    self.ts(ang[s], pf[s], invf, ALU.mult, [t_pf[s], self.t_c], [t_ang[s]])
                else:
                    self.ts(ang[s], pf[s], invf, ALU.mult, [t_pf[s], self.t_c], [t_ang[s]], s2=PI / 2, op1=ALU.add)
                self.ts(kf[s], ang[s], 1.0 / TWO_PI, ALU.mult, [t_ang[s]], [t_kf[s]])
                self.cp(ki[s], kf[s], [t_kf[s]], [t_ki[s]])
                self.cp(kf[s], ki[s], [t_ki[s]], [t_kf[s]])
                self.stt(ang[s], kf[s], -TWO_PI, ang[s], ALU.mult, ALU.add, [t_kf[s], t_ang[s]], [t_ang[s]])
                self.ts(gt[s], ang[s], PI, ALU.is_gt, [t_ang[s]], [t_gt[s]], s2=-TWO_PI, op1=ALU.mult)
                self.tt(ang[s], ang[s], gt[s], ALU.add, [t_ang[s], t_gt[s]], [t_ang[s]])
                self.ts(ang[s], ang[s], -PI, ALU.max, [t_ang[s]], [t_ang[s]], s2=PI, op1=ALU.min)
                self.act(res[r], ang[s], AF.Sin, [t_ang[s]], [t_res[r]])
                dst = self.sin_d if which == 0 else self.cos_d
                self.ld(dst[:, j * 512:(j + 1) * 512], res[r], [t_res[r]], [self.t_cs], nowaw=True)
        self.fw.barrier()

    def X3(self, ap):
        return ap.rearrange("(kc p) t -> p kc t", p=128)

    def load_x(self, src, j, slot):
        self.ld(self.xt[slot][:, :, :], self.X3(src)[:, :, j * 512:(j + 1) * 512],
                [self.t_X[j]], self.t_xt[slot])

    def store_x(self, j, slot):
        self.ld(self.X3(self.yT)[:, :, j * 512:(j + 1) * 512], self.xt[slot][:, :, :],
                self.t_xt[slot], [self.t_X[j]], nowaw=True)

    def norm(self, slot, gidx, bank=7):
        xt = self.xt[slot]
        ps = self.PS[bank]
        tps = self.t_ps[bank]
        n = self.norm_ctr
        self.norm_ctr += 1
        for kc in range(8):
            q = (n * 8 + kc) % 4
            self.act(self.sq[q][:, :], xt[:, kc, :], AF.Square, [self.t_xt[slot][kc]], [self.t_sq[q]])
            self.mm(ps[:, :], self.cb(C_ONE), self.sq[q][:, :], kc == 0, kc == 7, [self.t_sq[q], self.t_c], [tps])
        r = n % 2
        self.act(self.rt[r][:, :], ps[:, :], AF.Ln, [tps], [self.t_rt[r]], scale=1.0 / D, bias=EPS)
        self.act(self.rstd[r][:, :], self.rt[r][:, :], AF.Exp, [self.t_rt[r]], [self.t_rstd[r]], scale=-0.5)
        for kc in range(8):
            self.stt(self.h[slot][:, kc, :], xt[:, kc, :], self.gains[:, gidx * 8 + kc:gidx * 8 + kc + 1],
                     self.rstd[r][:, :], ALU.mult, ALU.mult,
                     [self.t_xt[slot][kc], self.t_rstd[r], self.t_c], [self.t_h[slot][kc]])

    def end_pass(self):
        self.fw.barrier()
        self.pass_count += 1
        return self.n_passes is not None and self.pass_count >= self.n_passes

    def ffn_pass(self, f, gidx, src):
        self.carve_reset()
        actT = [self.carve([128, 22, 512], BF16) for _ in range(2)]
        t_act = [[self.T("p1_") for _ in range(22)] for _ in range(2)]
        tmp = [self.carve([128, 512], F32) for _ in range(2)]
        t_tmp = [self.T("p2_") for _ in range(2)]
        wg = [self.carve([128, 8, 256], BF16) for _ in range(3)]
        wu = [self.carve([128, 8, 256], BF16) for _ in range(3)]
        t_wg = [self.T("p3_") for _ in range(3)]
        t_wu = [self.T("p4_") for _ in range(3)]
        wdn = [self.carve([128, 22, 256], BF16) for _ in range(2)]
        t_wdn = [self.T("p5_") for _ in range(2)]
        wgu3 = self.wgu_b[f].rearrange("(kc p) n -> p kc n", p=128)
        wd3 = self.wd_b[f].rearrange("(c p) n -> p c n", p=128)
        PS, tps = self.PS, self.t_ps
        wctr = 0
        dctr = 0
        ectr = 0
        for T in range(NT // 2):
            for s in range(2):
                self.load_x(src, 2 * T + s, s)
                self.norm(s, gidx)
            for g in range(11):
                ws = wctr % 3
                wctr += 1
                self.ld(wg[ws], wgu3[:, :, g * 256:(g + 1) * 256], [self.t_wgu_b[f]], [t_wg[ws]])
                self.ld(wu[ws], wgu3[:, :, DFF + g * 256:DFF + (g + 1) * 256], [self.t_wgu_b[f]], [t_wu[ws]])
                for s in range(2):
                    for cc in range(2):
                        c = 2 * g + cc
                        e = ectr % 2
                        ectr += 1
                        bg, bu = e, 2 + e
                        for kc in range(8):
                            self.mm(PS[bg][:, :], wg[ws][:, kc, cc * 128:(cc + 1) * 128], self.h[s][:, kc, :],
                                    kc == 0, kc == 7, [t_wg[ws], self.t_h[s][kc]], [tps[bg]])
                        for kc in range(8):
                            self.mm(PS[bu][:, :], wu[ws][:, kc, cc * 128:(cc + 1) * 128], self.h[s][:, kc, :],
                                    kc == 0, kc == 7, [t_wu[ws], self.t_h[s][kc]], [tps[bu]])
                        self.act(tmp[e], PS[bg][:, :], AF.Silu, [tps[bg]], [t_tmp[e]])
                        self.tt(actT[s][:, c, :], tmp[e], PS[bu][:, :], ALU.mult, [t_tmp[e], tps[bu]], [t_act[s][c]])
            for dg in range(4):
                ds_ = dctr % 2
                dctr += 1
                self.ld(wdn[ds_], wd3[:, :, dg * 256:(dg + 1) * 256], [self.t_wd_b[f]], [t_wdn[ds_]])
                for s in range(2):
                    for dd in range(2):
                        dc = 2 * dg + dd
                        e = ectr % 2
                        ectr += 1
                        by = 4 + e
                        for c in range(22):
                            self.mm(PS[by][:, :], wdn[ds_][:, c, dd * 128:(dd + 1) * 128], actT[s][:, c, :],
                                    c == 0, c == 21, [t_wdn[ds_], t_act[s][c]], [tps[by]])
                        self.stt(self.xt[s][:, dc, :], PS[by][:, :], 0.5, self.xt[s][:, dc, :], ALU.mult, ALU.add,
                                 [tps[by], self.t_xt[s][dc]], [self.t_xt[s][dc]])
            for s in range(2):
                self.store_x(2 * T + s, s)
        return self.end_pass()

    def outproj_pass(self, w_b, t_w):
        self.carve_reset()
        wout = self.carve([128, 8, 1024], BF16)
        t_wout = self.T("p6_")
        oTt = [self.carve([128, 8, 512], BF16) for _ in range(2)]
        t_oTt = [self.T("p7_") for _ in range(2)]
        self.ld(wout, w_b.rearrange("(c p) n -> p c n", p=128), [t_w], [t_wout])
        o3 = self.oT_d.rearrange("h p t -> p h t")
        PS, tps = self.PS, self.t_ps
        def prefetch(j):
            s = j % 2
            self.load_x(self.yT, j, s)
            self.ld(oTt[s], o3[:, :, j * 512:(j + 1) * 512], [self.t_oT[0]], [t_oTt[s]])

        prefetch(0)
        for j in range(NT):
            s = j % 2
            if j + 1 < NT:
                prefetch(j + 1)
            for dc in range(8):
                b = dc % 4
                for c in range(8):
                    self.mm(PS[b][:, :], wout[:, c, dc * 128:(dc + 1) * 128], oTt[s][:, c, :], c == 0, c == 7,
                            [t_wout, t_oTt[s]], [tps[b]])
                self.tt(self.xt[s][:, dc, :], PS[b][:, :], self.xt[s][:, dc, :], ALU.add,
                        [tps[b], self.t_xt[s][dc]], [self.t_xt[s][dc]])
            self.store_x(j, s)
        return self.end_pass()

    def sb_qkv_pass(self, sl, gidx):
        self.carve_reset()
        w = [self.carve([128, 8, 512], BF16) for _ in range(2)]
        t_w = [self.T("p8_") for _ in range(2)]
        st = [self.carve([128, 4, 512], BF16) for _ in range(2)]
        t_st = [[self.T("p9_") for _ in range(4)] for _ in range(2)]
        win3 = self.sbin_b[sl].rearrange("(kc p) n -> p kc n", p=128)
        q3 = self.qT_d.rearrange("h p t -> p h t")
        k3 = self.kT_d.rearrange("h p t -> p h t")
        v3 = self.v_d.rearrange("(b s) f -> s b f", s=128)
        PS, tps = self.PS, self.t_ps
        wctr = 0
        bctr = 0
        self.load_x(self.yT, 0, 0)
        for j in range(NT):
            s = j % 2
            self.norm(s, gidx)
            if j + 1 < NT:
                self.load_x(self.yT, j + 1, (j + 1) % 2)
            for grp in range(6):
                ws = wctr % 2
                wctr += 1
                self.ld(w[ws], win3[:, :, grp * 512:(grp + 1) * 512], [self.t_sbin_b[sl]], [t_w[ws]])
                ss = ws
                for i4 in range(4):
                    b = bctr % 4
                    bctr += 1
                    if grp < 4:
                        for kc in range(8):
                            self.mm(PS[b][:, :], w[ws][:, kc, i4 * 128:(i4 + 1) * 128], self.h[s][:, kc, :],
                                    kc == 0, kc == 7, [t_w[ws], self.t_h[s][kc]], [tps[b]])
                    else:
                        for kc in range(8):
                            self.mm(PS[b][:, :], self.h[s][:, kc, i4 * 128:(i4 + 1) * 128], w[ws][:, kc, :],
                                    kc == 0, kc == 7, [t_w[ws], self.t_h[s][kc]], [tps[b]])
                    self.evac(st[ss][:, i4, :], PS[b][:, :], [tps[b]], [t_st[ss][i4]])
                tsl = slice(j * 512, (j + 1) * 512)
                if grp < 2:
                    self.ld(q3[:, grp * 4:(grp + 1) * 4, tsl], st[ss], t_st[ss], [self.t_qT[0]], nowaw=True)
                elif grp < 4:
                    g2 = grp - 2
                    self.ld(k3[:, g2 * 4:(g2 + 1) * 4, tsl], st[ss], t_st[ss], [self.t_kT[0]], nowaw=True)
                else:
                    g2 = grp - 4
                    self.ld(v3[:, j * 4:(j + 1) * 4, g2 * 512:(g2 + 1) * 512], st[ss], t_st[ss], [self.t_v], nowaw=True)
        return self.end_pass()

    def sb_attn_pass(self):
        self.carve_reset()
        qT = self.carve([128, S], BF16)
        kT = self.carve([128, S], BF16)
        vp2 = self.carve([128, 32 * 2 * 128], BF16)
        vp = vp2.rearrange("p (a b c) -> p a b c", a=32, b=2)
        oTh = self.carve([128, S], BF16)
        NS = 4
        et = [self.carve([128, 512], F32) for _ in range(NS)]
        sp = [self.carve([128, 512], BF16) for _ in range(NS)]
        arg = [self.carve([128, 512], F32) for _ in range(NS)]
        Am = [self.carve([128, 512], BF16) for _ in range(NS)]
        Rb = [self.carve([128, 512], F32) for _ in range(2)]
        t_q, t_k, t_vp = self.T("p10_"), self.T("p11_"), self.T("p12_")
        t_o = [self.T("p13_") for _ in range(8)]
        t_et = [self.T("p14_") for _ in range(NS)]
        t_sp = [self.T("p15_") for _ in range(NS)]
        t_arg = [self.T("p16_") for _ in range(NS)]
        t_A = [self.T("p17_") for _ in range(NS)]
        t_R = [self.T("p18_") for _ in range(2)]
        PS, tps = self.PS, self.t_ps
        ident = self.cb(C_ID)
        tri = self.cb(C_TRI)
        neg1 = self.cb(C_NEG1)
        v3 = self.v_d.rearrange("(b s) f -> s b f", s=128)
        self.memset(vp2, 0.0, [t_vp])
        tc_ = self.t_c

        for hp in range(8):
            self.ld(qT, self.qT_d[hp], [self.t_qT[hp]], [t_q])
            self.ld(kT, self.kT_d[hp], [self.t_kT[hp]], [t_k])
            for e in range(2):
                for b0 in range(0, 32, 8):
                    self.ld(vp[:, b0:b0 + 8, e, e * 64:(e + 1) * 64],
                            v3[:, b0:b0 + 8, (2 * hp + e) * 64:(2 * hp + e + 1) * 64], [self.t_v], [t_vp])
            units = []
            for qg in range(8):
                for e in range(2):
                    kmax = 4 * qg + 3
                    for kb in range(kmax, -1, -1):
                        units.append((qg, e, kb))
            NU = len(units)

            NZ = 4

            def c0_of(u):
                qg, e, kb = units[u]
                m = kb - 4 * qg
                return 128 * m if m > 0 else 0

            def s1(u):
                qg, e, kb = units[u]
                rows = slice(e * 64, (e + 1) * 64)
                zb = u % NZ
                diag = kb >= 4 * qg
                c0 = c0_of(u)
                self.mm(PS[zb][:, c0:512], kT[rows, kb * 128:(kb + 1) * 128], qT[rows, qg * 512 + c0:(qg + 1) * 512],
                        True, False, [t_k, t_q], [tps[zb]], skip=True)
                if diag:
                    m = kb - 4 * qg
                    self.mm(PS[zb][:, c0:512], ident, self.cst[:, C_SBM + m * 512 + c0:C_SBM + (m + 1) * 512],
                            False, False, [tc_], [tps[zb]], skip=True)

            def s2a(u):
                zb = u % NZ
                sl_ = u % NS
                c0 = c0_of(u)
                self.act(et[sl_][:, c0:512], PS[zb][:, c0:512], AF.Exp, [tps[zb]], [t_et[sl_]], scale=0.125)

            def s2b(u):
                sl_ = u % NS
                c0 = c0_of(u)
                self.act(sp[sl_][:, c0:512], et[sl_][:, c0:512], AF.Ln, [t_et[sl_]], [t_sp[sl_]], bias=1.0)

            def s3(u):
                qg, e, kb = units[u]
                zb = u % NZ
                rbk = 4 + u % 2
                sl_ = u % NS
                c0 = c0_of(u)
                self.mm(PS[zb][:, c0:512], tri, sp[sl_][:, c0:512], False, True, [tc_, t_sp[sl_]], [tps[zb]], skip=True)
                if kb > 0:
                    self.mm(PS[rbk][:, c0:512], neg1, sp[sl_][:, c0:512], True, True, [tc_, t_sp[sl_]], [tps[rbk]])

            def s4(u):
                qg, e, kb = units[u]
                zb = u % NZ
                rbk = 4 + u % 2
                sl_ = u % NS
                gi = (qg * 2 + e) % 2
                first = kb == 4 * qg + 3
                c0 = c0_of(u)
                if first:
                    self.memset(Rb[gi][:, 0:c0], 0.0, [t_R[gi]], eng="dve")
                    self.ts(PS[zb][:, c0:512], PS[zb][:, c0:512], 0.125, ALU.mult, [tps[zb]], [tps[zb]])
                    self.cp(Rb[gi][:, c0:512], PS[rbk][:, c0:512], [tps[rbk]], [t_R[gi]])
                else:
                    self.stt(PS[zb][:, c0:512], PS[zb][:, c0:512], 0.125, Rb[gi][:, c0:512], ALU.mult, ALU.add,
                             [tps[zb], t_R[gi]], [tps[zb]])
                    if kb > 0:
                        self.tt(Rb[gi][:, c0:512], Rb[gi][:, c0:512], PS[rbk][:, c0:512], ALU.add,
                                [t_R[gi], tps[rbk]], [t_R[gi]])

            def s5(u):
                sl_ = u % NS
                c0 = c0_of(u)
                zb = u % NZ
                self.act(Am[sl_][:, c0:512], PS[zb][:, c0:512], AF.Exp, [tps[zb]], [t_A[sl_]])

            def s6(u):
                qg, e, kb = units[u]
                sl_ = u % NS
                ob = 6 + qg % 2
                c0 = c0_of(u)
                first = (e == 0 and kb == 4 * qg + 3)
                last = (e == 1 and kb == 0)
                self.mm(PS[ob][:, c0:512], vp[:, kb, e, :], Am[sl_][:, c0:512], first, last, [t_vp, t_A[sl_]], [tps[ob]],
                        skip=True)
                if last:
                    self.evac(oTh[:, qg * 512:(qg + 1) * 512], PS[ob][:, :], [tps[ob]], [t_o[qg]])

            s1(0)
            for tau in range(NU + 3):
                if tau + 1 < NU:
                    s1(tau + 1)
                if tau < NU:
                    s2a(tau)
                if 0 <= tau - 2 < NU:
                    s5(tau - 2)
                if tau < NU:
                    s2b(tau)
                if 0 <= tau - 1 < NU:
                    s3(tau - 1)
                    s4(tau - 1)
                if 0 <= tau - 3 < NU:
                    s6(tau - 3)
            self.ld(self.oT_d[hp], oTh, t_o, [self.t_oT[hp]], nowaw=True)
        return self.end_pass()

    def sw_qkv_pass(self, sl, gidx, gsl):
        self.carve_reset()
        NSL = 3
        win = self.carve([128, 8, 1280], BF16)
        kdw = self.carve([128, 8, 256], BF16)
        t_win, t_kdw = self.T("w"), self.T("w")
        cs = [self.carve([128, 512], F32) for _ in range(2)]
        sn = [self.carve([128, 512], F32) for _ in range(2)]
        t_cs = [self.T("w") for _ in range(2)]
        t_sn = [self.T("w") for _ in range(2)]
        qn = [self.carve([128, 512], F32) for _ in range(NSL)]
        qnb = [self.carve([128, 512], BF16) for _ in range(NSL)]
        sqq = [self.carve([128, 512], BF16) for _ in range(NSL)]
        rq = [self.carve([128, 512], F32) for _ in range(NSL)]
        t1 = [self.carve([128, 512], F32) for _ in range(NSL)]
        t2 = [self.carve([128, 512], F32) for _ in range(NSL)]
        t_qn = [self.T("w") for _ in range(NSL)]
        t_qnb = [self.T("w") for _ in range(NSL)]
        t_sqq = [self.T("w") for _ in range(NSL)]
        t_rq = [self.T("w") for _ in range(NSL)]
        t_t1 = [self.T("w") for _ in range(NSL)]
        t_t2 = [self.T("w") for _ in range(NSL)]
        st = [self.carve([128, 4, 512], BF16) for _ in range(3)]
        t_st = [[self.T("w") for _ in range(4)] for _ in range(3)]
        vst = [self.carve([128, 4, 128], BF16) for _ in range(2)]
        t_vst = [self.T("w") for _ in range(2)]
        win3 = self.swin_b[sl].rearrange("(kc p) n -> p kc n", p=128)
        self.ld(win, win3, [self.t_swin_b[sl]], [t_win])
        for g in range(2):
            for dup in range(2):
                c0 = (g * 2 + dup) * 64
                self.ld(kdw[:, :, c0:c0 + 64], win3[:, :, 1024 + g * 64:1024 + (g + 1) * 64],
                        [self.t_swin_b[sl]], [t_kdw])
        q3 = self.qT_d.rearrange("h p t -> p h t")
        k3 = self.kT_d.rearrange("h p t -> p h t")
        v3 = self.v2_d.rearrange("(b s) f -> s b f", s=128)
        PS, tps = self.PS, self.t_ps
        blk = self.cb(C_BLK)
        swp = self.cb(C_SWP)
        tc_ = self.t_c
        units = [(j, u) for j in range(NT) for u in range(10)]
        NU = len(units)

        def prep_tile(j):
            s = j % 2
            tsl = slice(j * 512, (j + 1) * 512)
            if j == 0:
                self.load_x(self.yT, 0, 0)
            self.norm(s, gidx)
            self.ld(cs[s], self.cos_d[:, tsl], [self.t_cs], [t_cs[s]])
            self.ld(sn[s], self.sin_d[:, tsl], [self.t_cs], [t_sn[s]])
            if j + 1 < NT:
                self.load_x(self.yT, j + 1, (j + 1) % 2)

        def do_v(j):
            s = j % 2
            for i4 in range(4):
                for kc in range(8):
                    self.mm(PS[7][:, i4 * 128:(i4 + 1) * 128], self.h[s][:, kc, i4 * 128:(i4 + 1) * 128],
                            win[:, kc, 1152:1280], kc == 0, kc == 7, [t_win, self.t_h[s][kc]], [tps[7]])
            self.evac(vst[s].rearrange("p a b -> p (a b)"), PS[7][:, :], [tps[7]], [t_vst[s]])
            self.ld(v3[:, j * 4:(j + 1) * 4, :], vst[s], [t_vst[s]], [self.t_v], nowaw=True)

        def p1(i):
            j, unit = units[i]
            s = j % 2
            if unit == 0:
                prep_tile(j)
            u = i % NSL
            b0 = u
            isq = unit < 8
            for kc in range(8):
                lhs = win[:, kc, unit * 128:(unit + 1) * 128] if isq else kdw[:, kc, (unit - 8) * 128:(unit - 7) * 128]
                self.mm(PS[b0][:, :], lhs, self.h[s][:, kc, :], kc == 0, kc == 7,
                        [t_win if isq else t_kdw, self.t_h[s][kc]], [tps[b0]])
            self.act(sqq[u], PS[b0][:, :], AF.Square, [tps[b0]], [t_sqq[u]])
            if unit == 9:
                do_v(j)

        def p2(i):
            j, unit = units[i]
            u = i % NSL
            b0 = u
            b1 = 3 + i % 2
            isq = unit < 8
            gc = 2 * gsl + (0 if isq else 1)
            gcol = self.qkg[:, gc:gc + 1]
            self.mm(PS[b1][:, :], blk, sqq[u], True, True, [tc_, t_sqq[u]], [tps[b1]])
            self.act(rq[u], PS[b1][:, :], AF.Ln, [tps[b1]], [t_rq[u]], scale=1.0 / 64, bias=EPS)
            self.act(rq[u], rq[u], AF.Exp, [t_rq[u]], [t_rq[u]], scale=-0.5)
            self.stt(qn[u], PS[b0][:, :], gcol, rq[u], ALU.mult, ALU.mult, [tps[b0], t_rq[u], tc_], [t_qn[u]])
            self.cp(qnb[u], qn[u], [t_qn[u]], [t_qnb[u]], eng="act")

        def p3(i):
            j, unit = units[i]
            s = j % 2
            tsl = slice(j * 512, (j + 1) * 512)
            u = i % NSL
            b2 = 5 + i % 2
            isq = unit < 8
            self.mm(PS[b2][:, :], swp, qnb[u], True, True, [tc_, t_qnb[u]], [tps[b2]])
            self.tt(t1[u], qn[u], cs[s], ALU.mult, [t_qn[u], t_cs[s]], [t_t1[u]])
            self.tt(t2[u], PS[b2][:, :], sn[s], ALU.mult, [tps[b2], t_sn[s]], [t_t2[u]])
            if isq:
                ss = (unit // 4 + 2 * j) % 2
                i4 = unit % 4
                self.tt(st[ss][:, i4, :], t1[u], t2[u], ALU.add, [t_t1[u], t_t2[u]], [t_st[ss][i4]])
                if i4 == 3:
                    g4 = unit // 4
                    self.ld(q3[:, g4 * 4:(g4 + 1) * 4, tsl], st[ss], t_st[ss], [self.t_qT[0]], nowaw=True)
            else:
                ss = 2
                i4 = unit - 8
                self.tt(st[ss][:, i4, :], t1[u], t2[u], ALU.add, [t_t1[u], t_t2[u]], [t_st[ss][i4]])
                if i4 == 1:
                    self.ld(k3[:, 0:2, tsl], st[ss][:, 0:2, :], t_st[ss][0:2], [self.t_kT[0]], nowaw=True)

        for tau in range(NU + 2):
            if tau < NU:
                p1(tau)
            if 0 <= tau - 1 < NU:
                p2(tau - 1)
            if 0 <= tau - 2 < NU:
                p3(tau - 2)
        return self.end_pass()

    def sw_attn_pass(self, sl):
        self.carve_reset()
        qT = [self.carve([128, S], BF16) for _ in range(2)]
        t_q = [self.T("w") for _ in range(2)]
        kTd = self.carve([128, 2, S], BF16)
        t_k = self.T("w")
        vpad2 = [[self.carve([128, 32 * 128], BF16) for _ in range(2)] for _ in range(2)]
        vpad = [[vpad2[g][e].rearrange("p (a b) -> p a b", a=32) for e in range(2)] for g in range(2)]
        t_vp = self.T("w")
        oTh = [self.carve([128, S], BF16) for _ in range(2)]
        t_o = [[self.T("w") for _ in range(8)] for _ in range(2)]
        NA = 4
        Am = [self.carve([128, 512], BF16) for _ in range(NA)]
        t_A = [self.T("w") for _ in range(NA)]
        den = [self.carve([128, 512], F32) for _ in range(2)]
        t_den = [self.T("w") for _ in range(2)]
        PS, tps = self.PS, self.t_ps
        SBK = [0, 1, 6, 7]
        ident = self.cb(C_ID)
        tc_ = self.t_c
        swm = self.cst[:, C_SWM:C_SWM + 512]
        onesE = [self.cb(C_E0), self.cb(C_E1)]
        v3 = self.v2_d.rearrange("(b s) f -> s b f", s=128)
        k3 = self.kT_d.rearrange("h p t -> p h t")
        for g in range(2):
            for e in range(2):
                self.memset(vpad2[g][e], 0.0, [t_vp])
        self.ld(kTd, k3[:, 0:2, :], [self.t_kT[0]], [t_k])
        for g in range(2):
            for e in range(2):
                for b0 in range(0, 32, 8):
                    self.ld(vpad[g][e][:, b0:b0 + 8, e * 64:(e + 1) * 64], v3[:, b0:b0 + 8, g * 64:(g + 1) * 64],
                            [self.t_v], [t_vp])
        units = [(hp, kb, e) for hp in range(8) for kb in range(32) for e in range(2)]
        NU = len(units)

        def s1(i):
            hp, kb, e = units[i]
            qs = hp % 2
            g = hp // 4
            if kb == 0 and e == 0:
                self.ld(qT[qs], self.qT_d[hp], [self.t_qT[hp]], [t_q[qs]])
            bk = SBK[i % 4]
            a_ = i % NA
            n = 256 if kb < 31 else 128
            rows = slice(e * 64, (e + 1) * 64)
            self.mm(PS[bk][:, 0:n], kTd[rows, g, kb * 128:(kb + 1) * 128], qT[qs][rows, kb * 128:kb * 128 + n],
                    True, False, [t_k, t_q[qs]], [tps[bk]])
            self.mm(PS[bk][:, 0:n], ident, swm[:, 0:n], False, True, [tc_], [tps[bk]])
            self.act(Am[a_][:, 0:n], PS[bk][:, 0:n], AF.Exp, [tps[bk]], [t_A[a_]], scale=0.125)

        def s3(i):
            hp, kb, e = units[i]
            qs = hp % 2
            g = hp // 4
            a_ = i % NA
            for part in range(2):
                qb = kb + part
                if qb > 31:
                    continue
                qg = qb // 4
                ob = 2 + qg % 2
                db = 4 + qg % 2
                cols = slice((qb % 4) * 128, (qb % 4 + 1) * 128)
                first = (part == 1 and e == 0 and qb % 4 == 0) or (qb == 0 and part == 0 and e == 0)
                last = (part == 0 and e == 1)
                asl = Am[a_][:, part * 128:(part + 1) * 128]
                self.mm(PS[ob][:, cols], vpad[g][e][:, kb, :], asl, first, last, [t_vp, t_A[a_]], [tps[ob]], skip=True)
                self.mm(PS[db][:, cols], onesE[e], asl, first, last, [tc_, t_A[a_]], [tps[db]], skip=True)
            if e == 1 and kb % 4 == 3:
                qg = kb // 4
                ob = 2 + qg % 2
                db = 4 + qg % 2
                d_ = qg % 2
                self.ts(den[d_], PS[db][:, :], self.sinkexp[:, sl * 8 + hp:sl * 8 + hp + 1], ALU.add,
                        [tps[db], tc_], [t_den[d_]])
                self.recip(den[d_], den[d_], [t_den[d_]], [t_den[d_]])
                self.tt(oTh[qs][:, qg * 512:(qg + 1) * 512], PS[ob][:, :], den[d_], ALU.mult,
                        [tps[ob], t_den[d_]], [t_o[qs][qg]])
            if e == 1 and kb == 31:
                self.ld(self.oT_d[hp], oTh[qs], t_o[qs], [self.t_oT[hp]], nowaw=True)

        for tau in range(NU + 2):
            if tau < NU:
                s1(tau)
            if 0 <= tau - 2 < NU:
                s3(tau - 2)
        return self.end_pass()

    def build(self):
        self.prologue()
        isb = isw = 0
        for idx, li in enumerate(self.layers):
            gsl = li // 2
            src = self.xT if idx == 0 else self.yT
            if self.ffn_pass(2 * idx, li * 3 + 0, src):
                break
            if li % 2 == 0:
                if self.sb_qkv_pass(isb, li * 3 + 1):
                    break
                if self.sb_attn_pass():
                    break
                if self.outproj_pass(self.sbout_b[isb], self.t_sbout_b[isb]):
                    break
                isb += 1
            else:
                if self.sw_qkv_pass(isw, li * 3 + 1, gsl):
                    break
                if self.sw_attn_pass(gsl):
                    break
                if self.outproj_pass(self.swout_b[isw], self.t_swout_b[isw]):
                    break
                isw += 1
            if self.ffn_pass(2 * idx + 1, li * 3 + 2, self.yT):
                break
        self.fw.barrier()
        self.fw.finalize()
        return self.nc


def make_consts():
    bf = ml_dtypes.bfloat16
    c = np.zeros((128, NCB), np.float32)
    idx = np.arange(128)
    c[:, C_ID:C_ID + 128] = np.eye(128)
    c[:, C_TRI:C_TRI + 128] = np.where(idx[:, None] >= idx[None, :], -8.0, 0.0)
    c[:, C_NEG1:C_NEG1 + 128] = -1.0
    c[:, C_ONE:C_ONE + 128] = 1.0
    c[:, C_BLK:C_BLK + 128] = (idx[:, None] // 64 == idx[None, :] // 64).astype(np.float32)
    swp = np.zeros((128, 128), np.float32)
    for m in range(128):
        d = m % 64
        if d < 8:
            swp[m + 8, m] = -1.0
        elif d < 16:
            swp[m - 8, m] = 1.0
    c[:, C_SWP:C_SWP + 128] = swp
    c[:, C_E0:C_E0 + 128] = (idx[None, :] < 64).astype(np.float32)
    c[:, C_E1:C_E1 + 128] = (idx[None, :] >= 64).astype(np.float32)
    t = np.arange(512)
    for m in range(4):
        allowed = (m * 128 + idx[:, None]) < t[None, :]
        c[:, C_SBM + m * 512:C_SBM + (m + 1) * 512] = np.where(allowed, 0.0, NEG)
    tl = np.arange(128)
    cur = idx[:, None] <= tl[None, :]
    prev = idx[:, None] > tl[None, :]
    for rep in range(2):
        c[:, C_SWM + rep * 256:C_SWM + rep * 256 + 128] = np.where(cur, 0.0, NEG)
        c[:, C_SWM + rep * 256 + 128:C_SWM + rep * 256 + 256] = np.where(prev, 0.0, NEG)
    cf = np.zeros((128, 4), np.float32)
    inv_freq = (500000.0 ** (-np.arange(0, 16, 2, dtype=np.float32) / 16)).astype(np.float32)
    for p in range(128):
        d = p % 64
        if d < 16:
            cf[p, 0] = inv_freq[d % 8]
    return c.astype(bf), cf


_NC_CACHE = {}

PLAN = [[0, 1, 2, 3]]


def get_nc(layers, n_passes=None):
    key = (tuple(layers), n_passes)
    if key not in _NC_CACHE:
        _NC_CACHE[key] = Builder(n_passes, layers).build()
    return _NC_CACHE[key]


def make_in_maps(inputs, xTs, cores, layers):
    pos = np.asarray(inputs["positions"]).astype(np.int32)
    cstb, cstf = make_consts()
    ng = np.asarray(inputs["norm_gains"], np.float32)
    gains = np.ascontiguousarray(ng.reshape(12, 8, 128).transpose(2, 0, 1).reshape(128, 96))
    qg = np.asarray(inputs["sw_q_gain"], np.float32)
    kg = np.asarray(inputs["sw_k_gain"], np.float32)
    qkg = np.stack([np.tile(qg[0], 2), np.tile(kg[0], 2), np.tile(qg[1], 2), np.tile(kg[1], 2)], axis=1)
    sk = np.asarray(inputs["sw_sinks"], np.float32)
    sinks = np.zeros((128, 16), np.float32)
    for sl in range(2):
        for hp in range(8):
            sinks[:64, sl * 8 + hp] = sk[sl, 2 * hp]
            sinks[64:, sl * 8 + hp] = sk[sl, 2 * hp + 1]
    wgu = np.asarray(inputs["ffn_w_gate_up"], np.float32)
    wd = np.asarray(inputs["ffn_w_down"], np.float32)
    shared = {
        "wgu": np.ascontiguousarray(np.concatenate([wgu[l] for l in layers], axis=0)),
        "wd": np.ascontiguousarray(np.concatenate([wd[l] for l in layers], axis=0)),
        "gains": gains, "qkg": np.ascontiguousarray(qkg), "sinks": sinks, "cstb": cstb, "cstf": cstf,
    }
    sbl = [l // 2 for l in layers if l % 2 == 0]
    swl = [l // 2 for l in layers if l % 2 == 1]
    if sbl:
        shared["sbin"] = np.ascontiguousarray(np.asarray(inputs["sb_w_in"], np.float32)[sbl])
        shared["sbout"] = np.ascontiguousarray(np.asarray(inputs["sb_w_out"], np.float32)[sbl])
    if swl:
        shared["swin"] = np.ascontiguousarray(np.asarray(inputs["sw_w_in"], np.float32)[swl])
        shared["swout"] = np.ascontiguousarray(np.asarray(inputs["sw_w_out"], np.float32)[swl])
    maps = []
    for i, b in enumerate(cores):
        m = dict(shared)
        m["xT"] = xTs[i]
        m["pos"] = np.ascontiguousarray(pos[b].reshape(1, S))
        maps.append(m)
    return maps


def kernel(x, positions, norm_gains, ffn_w_gate_up, ffn_w_down, sb_w_in, sb_w_out,
           sw_w_in, sw_w_out, sw_q_gain, sw_k_gain, sw_sinks):
    inputs = dict(x=x, positions=positions, norm_gains=norm_gains, ffn_w_gate_up=ffn_w_gate_up,
                  ffn_w_down=ffn_w_down, sb_w_in=sb_w_in, sb_w_out=sb_w_out, sw_w_in=sw_w_in,
                  sw_w_out=sw_w_out, sw_q_gain=sw_q_gain, sw_k_gain=sw_k_gain, sw_sinks=sw_sinks)
    x = np.asarray(x)
    cores = list(range(8))
    xTs = [np.ascontiguousarray(x[b].T) for b in cores]
    for layers in PLAN:
        nc = get_nc(layers)
        maps = make_in_maps(inputs, xTs, cores, layers)
        res = run_bass_kernel_spmd(nc, maps, core_ids=cores)
        xTs = [np.asarray(res.results[b]["yT"]) for b in cores]
    out = np.empty((8, S, D), np.float32)
    for b in cores:
        out[b] = xTs[b].T
    return out
```
